# Optimizing a Trainium2 kernel written in Bass

```python
import jax, jax.numpy as jnp
from jax import lax
import numpy as np

D_MODEL = 1024
BATCH = 2
SEQ = 8192
DEPTH = 4

N_MIXERS = 3
EXPAND = 2
D_INNER = EXPAND * D_MODEL
CONF_KERNEL = 31
SHORT_KERNEL = 3
FOX_HEADS = 16
FOX_HEAD_DIM = D_INNER // FOX_HEADS
Q_BLOCK = 128
NORM_EPS = 1e-6
N_A = (DEPTH + 2) // 3
N_B = (DEPTH + 1) // 3
N_C = DEPTH // 3

kernel_name = "hybrid_conformer_fox_shortconv_trunk"


def rms_norm(x, g):
    xf = x.astype(jnp.float32)
    y = xf * lax.rsqrt(jnp.mean(xf * xf, axis=-1, keepdims=True) + NORM_EPS)
    return (y * g.astype(jnp.float32)).astype(x.dtype)


def layer_norm(x, g, b):
    xf = x.astype(jnp.float32)
    mu = jnp.mean(xf, axis=-1, keepdims=True)
    var = jnp.mean(jnp.square(xf - mu), axis=-1, keepdims=True)
    y = (xf - mu) * lax.rsqrt(var + NORM_EPS)
    return (y * g.astype(jnp.float32) + b.astype(jnp.float32)).astype(x.dtype)


def causal_depthwise_conv(x, w):
    k_width, channels = w.shape
    kern = w[:, None, :].astype(x.dtype)
    return lax.conv_general_dilated(
        x, kern, window_strides=(1,), padding=[(k_width - 1, 0)],
        dimension_numbers=("NWC", "WIO", "NWC"), feature_group_count=channels)


def conformer_conv_mixer(h, w_in, conv_w, conv_b, ln_g, ln_b, w_out):
    proj = h @ w_in
    val, glu_gate, z = jnp.split(proj, 3, axis=-1)
    u = val * jax.nn.sigmoid(glu_gate)
    u = causal_depthwise_conv(u, conv_w) + conv_b
    u = jax.nn.silu(layer_norm(u, ln_g, ln_b))
    return (u * jax.nn.silu(z)) @ w_out


def blocked_forgetting_attention(q, k, v, c):
    b, s_len, n_h, d_h = q.shape
    n_blocks = s_len // Q_BLOCK
    scale = d_h ** -0.5
    k_pos = jnp.arange(s_len)

    def one_block(i):
        start = i * Q_BLOCK
        q_blk = lax.dynamic_slice_in_dim(q, start, Q_BLOCK, axis=1)
        c_blk = lax.dynamic_slice_in_dim(c, start, Q_BLOCK, axis=2)
        logits = jnp.einsum("bqhd,bkhd->bhqk", q_blk, k,
                            preferred_element_type=jnp.float32) * scale
        logits = logits + c_blk[..., :, None] - c[..., None, :]
        q_pos = start + jnp.arange(Q_BLOCK)
        causal = k_pos[None, :] <= q_pos[:, None]
        logits = jnp.where(causal, logits, -jnp.inf)
        p = jax.nn.softmax(logits, axis=-1)
        return jnp.einsum("bhqk,bkhd->bqhd", p.astype(v.dtype), v)

    out = lax.map(one_block, jnp.arange(n_blocks))
    return out.transpose(1, 0, 2, 3, 4).reshape(b, s_len, n_h, d_h)


def forgetting_attention_mixer(h, w_in, f_bias, q_norm_g, k_norm_g, w_out):
    b, s_len, _ = h.shape
    proj = h @ w_in
    q, k, v, z, f_logit = jnp.split(
        proj, [D_INNER, 2 * D_INNER, 3 * D_INNER, 4 * D_INNER], axis=-1)
    q = rms_norm(q.reshape(b, s_len, FOX_HEADS, FOX_HEAD_DIM), q_norm_g)
    k = rms_norm(k.reshape(b, s_len, FOX_HEADS, FOX_HEAD_DIM), k_norm_g)
    v = v.reshape(b, s_len, FOX_HEADS, FOX_HEAD_DIM)
    log_f = jax.nn.log_sigmoid((f_logit + f_bias).astype(jnp.float32))
    c = jnp.cumsum(log_f, axis=1).transpose(0, 2, 1)
    o = blocked_forgetting_attention(q, k, v, c).reshape(b, s_len, D_INNER)
    return (o * jax.nn.silu(z)) @ w_out


def short_conv_mixer(h, w_in, conv_w, w_out):
    proj = h @ w_in
    u, b_gate, c_gate, z = jnp.split(proj, 4, axis=-1)
    y = b_gate * causal_depthwise_conv(c_gate * u, conv_w)
    return (y * jax.nn.silu(z)) @ w_out


def setup_inputs(seed: int = 0) -> dict:
    key = jax.random.key(seed)
    ks = jax.random.split(key, 20)
    f32 = jnp.float32
    nrm = lambda k, shape, s: jax.random.normal(k, shape, f32) * s
    d, e, h = D_MODEL, D_INNER, FOX_HEADS
    return {
        "x": jax.random.normal(ks[0], (BATCH, SEQ, d), f32),
        "a_norm": 1.0 + nrm(ks[1], (N_A, d), 0.05),
        "a_w_in": nrm(ks[2], (N_A, d, 3 * e), d ** -0.5),
        "a_conv_w": nrm(ks[3], (N_A, CONF_KERNEL, e), CONF_KERNEL ** -0.5),
        "a_conv_b": nrm(ks[4], (N_A, e), 0.01),
        "a_ln_g": 1.0 + nrm(ks[5], (N_A, e), 0.05),
        "a_ln_b": nrm(ks[6], (N_A, e), 0.01),
        "a_w_out": nrm(ks[7], (N_A, e, d), e ** -0.5),
        "b_norm": 1.0 + nrm(ks[8], (N_B, d), 0.05),
        "b_w_in": nrm(ks[9], (N_B, d, 4 * e + h), d ** -0.5),
        "b_f_bias": 2.0 + nrm(ks[10], (N_B, h), 0.5),
        "b_q_norm": 1.0 + nrm(ks[11], (N_B, FOX_HEAD_DIM), 0.05),
        "b_k_norm": 1.0 + nrm(ks[12], (N_B, FOX_HEAD_DIM), 0.05),
        "b_w_out": nrm(ks[13], (N_B, e, d), e ** -0.5),
        "c_norm": 1.0 + nrm(ks[14], (N_C, d), 0.05),
        "c_w_in": nrm(ks[15], (N_C, d, 4 * e), d ** -0.5),
        "c_conv_w": nrm(ks[16], (N_C, SHORT_KERNEL, e), SHORT_KERNEL ** -0.5),
        "c_w_out": nrm(ks[17], (N_C, e, d), e ** -0.5),
    }


def reference(x, a_norm, a_w_in, a_conv_w, a_conv_b, a_ln_g, a_ln_b, a_w_out,
              b_norm, b_w_in, b_f_bias, b_q_norm, b_k_norm, b_w_out,
              c_norm, c_w_in, c_conv_w, c_w_out):
    for i in range(DEPTH):
        kind, j = i % N_MIXERS, i // N_MIXERS
        if kind == 0:
            hn = rms_norm(x, a_norm[j])
            x = x + conformer_conv_mixer(hn, a_w_in[j], a_conv_w[j], a_conv_b[j],
                                         a_ln_g[j], a_ln_b[j], a_w_out[j])
        elif kind == 1:
            hn = rms_norm(x, b_norm[j])
            x = x + forgetting_attention_mixer(hn, b_w_in[j], b_f_bias[j], b_q_norm[j],
                                               b_k_norm[j], b_w_out[j])
        else:
            hn = rms_norm(x, c_norm[j])
            x = x + short_conv_mixer(hn, c_w_in[j], c_conv_w[j], c_w_out[j])
    return x
```

```python
import numpy as np
from contextlib import ExitStack
import concourse.bass as bass
import concourse.mybir as mybir
from concourse.bass_utils import run_bass_kernel_spmd

F32 = mybir.dt.float32
BF16 = mybir.dt.bfloat16
AF = mybir.ActivationFunctionType
ALU = mybir.AluOpType

NCORES = 8
D = 1024
E = 2048
SEQ = 8192
BATCH = 2
NTOK = BATCH * SEQ
TPC = NTOK // NCORES
TT = 512
NT = TPC // TT
KD = D // 128
KE = E // 128
EPS = 1e-6
HEADS = 16

SAME_ENGINE_RAW_SYNC = True


class T:
    __slots__ = ("ap", "name", "w", "r", "dsem", "dcount")

    def __init__(self, ap, name=""):
        self.ap = ap
        self.name = name
        self.w = None
        self.r = []
        self.dsem = None
        self.dcount = 0


class Op:
    __slots__ = ("fn", "deps", "signal", "dma", "sigval")

    def __init__(self, fn, deps, dma):
        self.fn = fn
        self.deps = deps
        self.signal = False
        self.dma = dma
        self.sigval = None


class Prog:
    ENGS = ("pe", "act", "dve", "pool", "sp")

    def __init__(self, nc):
        self.nc = nc
        self.ops = {e: [] for e in self.ENGS}
        self.out_events = []
        self.n_dsem = 0

    def _deps(self, eng, reads, writes):
        deps = []
        for t in reads:
            if t.w is not None:
                deps.append((t.w, "raw"))
        for t in writes:
            if t.w is not None:
                deps.append((t.w, "waw"))
            for ev in t.r:
                deps.append((ev, "war"))
        out = set()
        for ev, kind in deps:
            if ev[0] == "eng" and ev[1] == eng:
                if kind == "raw" and SAME_ENGINE_RAW_SYNC and eng != "pe":
                    out.add(ev)
                continue
            out.add(ev)
        return out

    def _update(self, ev, reads, writes):
        for t in reads:
            if ev[0] == "eng":
                t.r = [e for e in t.r if not (e[0] == "eng" and e[1] == ev[1])]
            t.r.append(ev)
        for t in writes:
            t.w = ev
            t.r = []

    def op(self, eng, fn, reads=(), writes=()):
        deps = self._deps(eng, reads, writes)
        self.ops[eng].append(Op(fn, deps, None))
        ev = ("eng", eng, len(self.ops[eng]) - 1)
        self._update(ev, reads, writes)
        return ev

    def dma(self, eng, out_t, in_t, out_ap, in_ap, semtile=None, is_output=False, **kw):
        st = semtile if semtile is not None else out_t
        if st.dsem is None:
            st.dsem = self.nc.alloc_semaphore(name=f"d{self.n_dsem}")
            self.n_dsem += 1
        st.dcount += 16
        sem, val = st.dsem, st.dcount
        deps = self._deps(eng, [in_t], [out_t])
        fn = lambda e, out_ap=out_ap, in_ap=in_ap, kw=kw: e.dma_start(out=out_ap, in_=in_ap, **kw)
        self.ops[eng].append(Op(fn, deps, (sem, val)))
        ev = ("dma", sem, val)
        self._update(ev, [in_t], [out_t])
        if is_output:
            self.out_events.append(ev)
        return ev

    def emit(self):
        nc = self.nc
        for e in self.ENGS:
            for o in self.ops[e]:
                for d in o.deps:
                    if d[0] == "eng":
                        self.ops[d[1]][d[2]].signal = True
        for e in self.ENGS:
            c = 0
            for o in self.ops[e]:
                if o.signal:
                    c += 1
                    o.sigval = c
        esem = {e: nc.alloc_semaphore(name=f"s_{e}") for e in self.ENGS}
        ops = self.ops
        out_events = self.out_events

        def run(ename, eng):
            waited = {}
            for o in ops[ename]:
                for d in o.deps:
                    if d[0] == "eng":
                        sem = esem[d[1]]
                        val = ops[d[1]][d[2]].sigval
                        key = ("e", d[1])
                    else:
                        sem, val = d[1], d[2]
                        key = ("d", id(sem))
                    if waited.get(key, 0) >= val:
                        continue
                    waited[key] = val
                    eng.wait_ge(sem, val)
                inst = o.fn(eng)
                if o.dma is not None:
                    inst.then_inc(o.dma[0], 16)
                elif o.signal:
                    inst.then_inc(esem[ename], 1)
            if ename == "sp":
                for d in out_events:
                    eng.wait_ge(d[1], d[2])

        with nc.Block() as block:
            @block.tensor
            def _(eng):
                run("pe", eng)

            @block.scalar
            def _(eng):
                run("act", eng)

            @block.vector
            def _(eng):
                run("dve", eng)

            @block.gpsimd
            def _(eng):
                run("pool", eng)

            @block.sync
            def _(eng):
                run("sp", eng)


class Ctx:
    def __init__(self, nc, es):
        self.nc = nc
        self.es = es
        self.P = Prog(nc)
        self.nm = 0
        self.ps = [self.psum(f"psb{i}") for i in range(8)]
        self.psi = 0

    def sb(self, shape, dt, name=None):
        self.nm += 1
        name = "sb_" + (name or f"t{self.nm}")
        ap = self.es.enter_context(self.nc.sbuf_tensor(name, list(shape), dt))
        return T(ap, name)

    def psum(self, name):
        ap = self.es.enter_context(self.nc.psum_tensor(name, [128, 512], F32))
        return T(ap, name)

    def dram(self, name, shape, dt, kind):
        return T(self.nc.dram_tensor(name, list(shape), dt, kind=kind).ap(), name)

    def nextps(self, nrot=6):
        t = self.ps[self.psi % nrot]
        self.psi += 1
        return t


class Rot:
    def __init__(self, tiles):
        self.tiles = tiles
        self.i = 0

    def next(self):
        t = self.tiles[self.i % len(self.tiles)]
        self.i += 1
        return t


def stage_consts(C):
    P = C.P
    ones = C.sb([128, 128], BF16, "ones")
    P.op("dve", lambda e: e.memset(ones.ap[:], 1.0), writes=[ones])
    C.ones = ones


def stage_rms_hn(C, xt, ncols, gcol, hn, sq, rstd):
    P = C.P
    ps = C.nextps()
    for k in range(KD):
        P.op("act", lambda e, k=k: e.activation(out=sq.ap[:, k, 0:ncols], in_=xt.ap[:, k, 0:ncols], func=AF.Square),
             reads=[xt], writes=[sq])
    for k in range(KD):
        P.op("pe", lambda e, k=k: e.matmul(ps.ap[:, 0:ncols], lhsT=C.ones.ap[:], rhs=sq.ap[:, k, 0:ncols],
                                           start=(k == 0), stop=(k == KD - 1)),
             reads=[C.ones, sq], writes=[ps])
    P.op("act", lambda e: e.activation(out=rstd.ap[:, 0:ncols], in_=ps.ap[:, 0:ncols], func=AF.Ln, scale=1.0 / D, bias=C.epsc.ap[:, 0:1]),
         reads=[ps, C.epsc], writes=[rstd])
    P.op("act", lambda e: e.activation(out=rstd.ap[:, 0:ncols], in_=rstd.ap[:, 0:ncols], func=AF.Exp, scale=-0.5),
         reads=[rstd], writes=[rstd])
    gt, goff = gcol
    for k in range(KD):
        P.op("dve", lambda e, k=k: e.scalar_tensor_tensor(out=hn.ap[:, k, 0:ncols], in0=xt.ap[:, k, 0:ncols],
                                                         scalar=gt.ap[:, goff + k:goff + k + 1], in1=rstd.ap[:, 0:ncols],
                                                         op0=ALU.mult, op1=ALU.mult),
             reads=[xt, gt, rstd], writes=[hn])


def load_wslab(C, wrot, wdram, col0, ncol, kchunks):
    wt = wrot.next()
    src = wdram.ap[:, col0:col0 + ncol].rearrange("(k p) n -> p k n", p=128)
    C.P.dma("pool", wt, wdram, wt.ap[:, 0:kchunks, 0:ncol], src)
    return wt


def mm_group(C, ps, pscols, wt, wcol0, rhs_t, rhs_fn, kchunks):
    for k in range(kchunks):
        C.P.op("pe", lambda e, k=k: e.matmul(ps.ap[:, pscols[0]:pscols[1]], lhsT=wt.ap[:, k, wcol0:wcol0 + 128], rhs=rhs_fn(k),
                                             start=(k == 0), stop=(k == kchunks - 1)),
               reads=[wt, rhs_t], writes=[ps])


def stage_out_proj(C, wout_dram, wrot, g, xt, xo, ncols=TT):
    P = C.P
    SL = 256
    for s in range(D // SL):
        wt = load_wslab(C, wrot, wout_dram, s * SL, SL, KE)
        for j in range(SL // 128):
            n = s * (SL // 128) + j
            ps = C.nextps()
            mm_group(C, ps, (0, ncols), wt, j * 128, g, lambda k: g.ap[:, k, 0:ncols], KE)
            P.op("dve", lambda e, n=n, ps=ps: e.tensor_tensor(out=xo.ap[:, n, 0:ncols], in0=ps.ap[:, 0:ncols], in1=xt.ap[:, n, 0:ncols], op=ALU.add),
                 reads=[ps, xt], writes=[xo])


def build_conv_layer(kind):
    KW = 31 if kind == "A" else 3
    H = KW - 1
    HP = 32 if kind == "A" else 2
    NIN = 3 * E if kind == "A" else 4 * E
    nc = bass.Bass("TRN2", target_bir_lowering=False)
    with ExitStack() as es:
        C = Ctx(nc, es)
        P = C.P
        O_G, O_CW = 0, KD
        O_CB = O_CW + KE * KW
        O_LG = O_CB + KE
        O_LB = O_LG + KE
        NP = O_LB + KE
        xT = C.dram("xT", [D, HP + TPC], F32, "ExternalInput")
        w_in = C.dram("w_in", [D, NIN], F32, "ExternalInput")
        w_out = C.dram("w_out", [E, D], F32, "ExternalInput")
        prm_d = C.dram("prm", [128, NP], F32, "ExternalInput")
        ident_d = C.dram("ident", [128, 128], F32, "ExternalInput")
        yT = C.dram("yT", [D, TPC], F32, "ExternalOutput")

        stage_consts(C)
        prm = C.sb([128, NP], F32, "prm")
        P.dma("sp", prm, prm_d, prm.ap[:], prm_d.ap[:])
        ident = C.sb([128, 128], BF16, "ident")
        P.dma("pool", ident, ident_d, ident.ap[:], ident_d.ap[:])
        epsc = C.sb([128, 1], F32, "epsc")
        P.op("dve", lambda e: e.memset(epsc.ap[:], EPS), writes=[epsc])
        C.epsc = epsc

        xrot = Rot([C.sb([128, KD, TT], F32, f"x{i}") for i in range(2)])
        hn = C.sb([128, KD, TT], BF16, "hn")
        sq = C.sb([128, KD, TT], BF16, "sq")
        rstd = C.sb([128, TT], F32, "rstd")
        xh = C.sb([128, KD, HP], F32, "xh")
        hnh = C.sb([128, KD, HP], BF16, "hnh")
        sqh = C.sb([128, KD, HP], BF16, "sqh")
        rstdh = C.sb([128, HP], F32, "rstdh")
        urot = Rot([C.sb([128, KE, HP + TT], BF16, f"u{i}") for i in range(1)])
        sz = C.sb([128, KE, TT], BF16, "sz")
        g = C.sb([128, KE, TT], BF16, "g")
        tmpr = Rot([C.sb([128, TT], F32, f"tmp{i}") for i in range(4)])
        tmph = C.sb([128, HP], F32, "tmph")
        WSL = 512
        wrot = Rot([C.sb([128, KD, WSL], BF16, f"w{i}") for i in range(4)])
        worot = Rot([C.sb([128, KE, 256], BF16, f"wo{i}") for i in range(2)])
        dgrot = Rot([C.sb([128, KW, 128], BF16, f"dg{i}") for i in range(2)])
        if kind == "A":
            vb = C.sb([128, KE, TT], BF16, "vb")
            vsq = g
            mu = C.sb([128, TT], F32, "mu")
            lr = C.sb([128, TT], F32, "lr")
            psS, psQ = C.ps[6], C.ps[7]
        else:
            bsb = C.sb([128, KE, TT], BF16, "bsb")

        blkA, blkB = (0, 1) if kind == "A" else (0, 2)

        def evac_pair(psa, psb, cols, tmp_ap, out_ap, reads_extra=()):
            fn = AF.Sigmoid if kind == "A" else AF.Copy
            return fn

        ucur = urot.next()
        P.dma("sp", xh, xT, xh.ap[:], xT.ap[:, 0:HP].rearrange("(k p) n -> p k n", p=128))
        stage_rms_hn(C, xh, HP, (prm, O_G), hnh, sqh, rstdh)
        for s in range(E // WSL):
            wa = load_wslab(C, wrot, w_in, blkA * E + s * WSL, WSL, KD)
            wb = load_wslab(C, wrot, w_in, blkB * E + s * WSL, WSL, KD)
            for j in range(WSL // 128):
                ch = s * (WSL // 128) + j
                psa, psb = C.nextps(), C.nextps()
                mm_group(C, psa, (0, HP), wa, j * 128, hnh, lambda k: hnh.ap[:, k, 0:HP], KD)
                mm_group(C, psb, (0, HP), wb, j * 128, hnh, lambda k: hnh.ap[:, k, 0:HP], KD)
                if kind == "A":
                    P.op("act", lambda e, psb=psb: e.activation(out=tmph.ap[:], in_=psb.ap[:, 0:HP], func=AF.Sigmoid), reads=[psb], writes=[tmph])
                else:
                    P.op("act", lambda e, psb=psb: e.activation(out=tmph.ap[:], in_=psb.ap[:, 0:HP], func=AF.Copy), reads=[psb], writes=[tmph])
                P.op("dve", lambda e, psa=psa, ch=ch, ucur=ucur: e.tensor_tensor(out=ucur.ap[:, ch, 0:HP], in0=psa.ap[:, 0:HP], in1=tmph.ap[:], op=ALU.mult),
                     reads=[psa, tmph], writes=[ucur])

        for t in range(NT):
            xt = xrot.next()
            xo = xt
            P.dma("sp", xt, xT, xt.ap[:], xT.ap[:, HP + t * TT:HP + (t + 1) * TT].rearrange("(k p) n -> p k n", p=128))
            stage_rms_hn(C, xt, TT, (prm, O_G), hn, sq, rstd)
            hnf = lambda k: hn.ap[:, k, :]
            for s in range(E // WSL):
                wa = load_wslab(C, wrot, w_in, blkA * E + s * WSL, WSL, KD)
                wb = load_wslab(C, wrot, w_in, blkB * E + s * WSL, WSL, KD)
                for j in range(WSL // 128):
                    ch = s * (WSL // 128) + j
                    psa, psb = C.nextps(), C.nextps()
                    mm_group(C, psa, (0, TT), wa, j * 128, hn, hnf, KD)
                    mm_group(C, psb, (0, TT), wb, j * 128, hn, hnf, KD)
                    tmp = tmpr.next()
                    fn = AF.Sigmoid if kind == "A" else AF.Copy
                    P.op("act", lambda e, psb=psb, tmp=tmp, fn=fn: e.activation(out=tmp.ap[:], in_=psb.ap[:], func=fn), reads=[psb], writes=[tmp])
                    P.op("dve", lambda e, psa=psa, tmp=tmp, ch=ch, ucur=ucur: e.tensor_tensor(out=ucur.ap[:, ch, HP:HP + TT], in0=psa.ap[:], in1=tmp.ap[:], op=ALU.mult),
                         reads=[psa, tmp], writes=[ucur])
            zblk = 2 if kind == "A" else 3
            for s in range(E // WSL):
                wz = load_wslab(C, wrot, w_in, zblk * E + s * WSL, WSL, KD)
                wbg = load_wslab(C, wrot, w_in, 1 * E + s * WSL, WSL, KD) if kind == "C" else None
                for j in range(WSL // 128):
                    ch = s * (WSL // 128) + j
                    psz = C.nextps()
                    mm_group(C, psz, (0, TT), wz, j * 128, hn, hnf, KD)
                    P.op("act", lambda e, psz=psz, ch=ch: e.activation(out=sz.ap[:, ch, :], in_=psz.ap[:], func=AF.Silu), reads=[psz], writes=[sz])
                    if kind == "C":
                        psg = C.nextps()
                        mm_group(C, psg, (0, TT), wbg, j * 128, hn, hnf, KD)
                        P.op("act", lambda e, psg=psg, ch=ch: e.activation(out=bsb.ap[:, ch, :], in_=psg.ap[:], func=AF.Copy), reads=[psg], writes=[bsb])
            for ch in range(KE):
                dg = dgrot.next()
                for k in range(KW):
                    P.op("dve", lambda e, k=k, ch=ch, dg=dg: e.tensor_scalar(out=dg.ap[:, k, :], in0=ident.ap[:], scalar1=prm.ap[:, O_CW + ch * KW + k:O_CW + ch * KW + k + 1],
                                                                       scalar2=None, op0=ALU.mult),
                         reads=[ident, prm], writes=[dg])
                psc = C.nextps()
                for k in range(KW):
                    off = HP - H + k
                    P.op("pe", lambda e, k=k, ch=ch, dg=dg, psc=psc, off=off, ucur=ucur: e.matmul(psc.ap[:], lhsT=dg.ap[:, k, :], rhs=ucur.ap[:, ch, off:off + TT],
                                                                                                 start=(k == 0), stop=(k == KW - 1)),
                         reads=[dg, ucur], writes=[psc])
                if kind == "A":
                    P.op("act", lambda e, ch=ch, psc=psc: e.activation(out=vb.ap[:, ch, :], in_=psc.ap[:], func=AF.Identity, bias=prm.ap[:, O_CB + ch:O_CB + ch + 1]),
                         reads=[psc, prm], writes=[vb])
                    P.op("act", lambda e, ch=ch, psc=psc: e.activation(out=vsq.ap[:, ch, :], in_=psc.ap[:], func=AF.Square, bias=prm.ap[:, O_CB + ch:O_CB + ch + 1]),
                         reads=[psc, prm], writes=[vsq])
                    P.op("pe", lambda e, ch=ch: e.matmul(psS.ap[:], lhsT=C.ones.ap[:], rhs=vb.ap[:, ch, :], start=(ch == 0), stop=(ch == KE - 1)),
                         reads=[C.ones, vb], writes=[psS])
                    P.op("pe", lambda e, ch=ch: e.matmul(psQ.ap[:], lhsT=C.ones.ap[:], rhs=vsq.ap[:, ch, :], start=(ch == 0), stop=(ch == KE - 1)),
                         reads=[C.ones, vsq], writes=[psQ])
                else:
                    tmp = tmpr.next()
                    P.op("dve", lambda e, ch=ch, psc=psc, tmp=tmp: e.tensor_tensor(out=tmp.ap[:], in0=psc.ap[:], in1=bsb.ap[:, ch, :], op=ALU.mult),
                         reads=[psc, bsb], writes=[tmp])
                    P.op("pool", lambda e, ch=ch, tmp=tmp: e.tensor_tensor(out=g.ap[:, ch, :], in0=tmp.ap[:], in1=sz.ap[:, ch, :], op=ALU.mult),
                         reads=[tmp, sz], writes=[g])
            if kind == "A":
                P.op("act", lambda e: e.activation(out=mu.ap[:], in_=psS.ap[:], func=AF.Copy, scale=1.0 / E), reads=[psS], writes=[mu])
                tmp = tmpr.next()
                P.op("dve", lambda e, tmp=tmp: e.tensor_tensor(out=tmp.ap[:], in0=mu.ap[:], in1=mu.ap[:], op=ALU.mult), reads=[mu], writes=[tmp])
                P.op("dve", lambda e, tmp=tmp: e.scalar_tensor_tensor(out=lr.ap[:], in0=psQ.ap[:], scalar=1.0 / E, in1=tmp.ap[:], op0=ALU.mult, op1=ALU.subtract),
                     reads=[psQ, tmp], writes=[lr])
                P.op("act", lambda e: e.activation(out=lr.ap[:], in_=lr.ap[:], func=AF.Ln, bias=C.epsc.ap[:, 0:1]), reads=[lr, C.epsc], writes=[lr])
                P.op("act", lambda e: e.activation(out=lr.ap[:], in_=lr.ap[:], func=AF.Exp, scale=-0.5), reads=[lr], writes=[lr])
                for ch in range(KE):
                    t1, t2 = tmpr.next(), tmpr.next()
                    P.op("dve", lambda e, ch=ch, t1=t1: e.tensor_tensor(out=t1.ap[:], in0=vb.ap[:, ch, :], in1=mu.ap[:], op=ALU.subtract), reads=[vb, mu], writes=[t1])
                    P.op("dve", lambda e, t1=t1: e.tensor_tensor(out=t1.ap[:], in0=t1.ap[:], in1=lr.ap[:], op=ALU.mult), reads=[t1, lr], writes=[t1])
                    P.op("act", lambda e, ch=ch, t1=t1, t2=t2: e.activation(out=t2.ap[:], in_=t1.ap[:], func=AF.Silu, scale=prm.ap[:, O_LG + ch:O_LG + ch + 1],
                                                                       bias=prm.ap[:, O_LB + ch:O_LB + ch + 1]),
                         reads=[t1, prm], writes=[t2])
                    P.op("pool", lambda e, ch=ch, t2=t2: e.tensor_tensor(out=g.ap[:, ch, :], in0=t2.ap[:], in1=sz.ap[:, ch, :], op=ALU.mult),
                         reads=[t2, sz], writes=[g])
            unext = urot.next()
            if t < NT - 1:
                P.op("pool", lambda e, ucur=ucur, unext=unext: e.tensor_copy(out=unext.ap[:, :, 0:HP], in_=ucur.ap[:, :, TT:TT + HP]), reads=[ucur], writes=[unext])
            ucur = unext
            stage_out_proj(C, w_out, worot, g, xt, xo)
            P.dma("sp", yT, xo, yT.ap[:, t * TT:(t + 1) * TT].rearrange("(k p) n -> p k n", p=128), xo.ap[:], semtile=xo, is_output=True)
        P.emit()
    return nc


def build_b1():
    nc = bass.Bass("TRN2", target_bir_lowering=False)
    NIN = 4 * E + HEADS
    with ExitStack() as es:
        C = Ctx(nc, es)
        P = C.P
        xT = C.dram("xT", [D, TPC], F32, "ExternalInput")
        w_in = C.dram("w_in", [D, NIN], F32, "ExternalInput")
        prm_d = C.dram("prm", [128, KD + 2], F32, "ExternalInput")
        fb_d = C.dram("fb", [HEADS, 1], F32, "ExternalInput")
        qT = C.dram("qT", [E, TPC], BF16, "ExternalOutput")
        kT = C.dram("kT", [E, TPC], BF16, "ExternalOutput")
        vO = C.dram("v", [TPC, E], BF16, "ExternalOutput")
        szT = C.dram("szT", [E, TPC], BF16, "ExternalOutput")
        lfO = C.dram("lf", [HEADS, TPC], F32, "ExternalOutput")
        stage_consts(C)
        prm = C.sb([128, KD + 2], F32, "prm")
        P.dma("sp", prm, prm_d, prm.ap[:], prm_d.ap[:])
        fb = C.sb([HEADS, 1], F32, "fb")
        P.dma("sp", fb, fb_d, fb.ap[:], fb_d.ap[:])
        epsc = C.sb([128, 1], F32, "epsc")
        P.op("dve", lambda e: e.memset(epsc.ap[:], EPS), writes=[epsc])
        C.epsc = epsc
        wf = C.sb([128, KD, HEADS], BF16, "wf")
        P.dma("pool", wf, w_in, wf.ap[:], w_in.ap[:, 4 * E:4 * E + HEADS].rearrange("(k p) n -> p k n", p=128))
        xrot = Rot([C.sb([128, KD, TT], F32, f"x{i}") for i in range(2)])
        hn = C.sb([128, KD, TT], BF16, "hn")
        sq = C.sb([128, KD, TT], BF16, "sq")
        rstd = C.sb([128, TT], F32, "rstd")
        WSL = 512
        wrot = Rot([C.sb([128, KD, WSL], BF16, f"w{i}") for i in range(4)])
        obrot = Rot([C.sb([128, KE, TT], BF16, f"ob{i}") for i in range(2)])
        vst = C.sb([128, 4, E], BF16, "vst")
        sqh = Rot([C.sb([128, TT], BF16, f"sqh{i}") for i in range(2)])
        tmpr = Rot([C.sb([128, TT], F32, f"tmp{i}") for i in range(3)])
        t16 = C.sb([HEADS, TT], F32, "t16")
        lfs = C.sb([HEADS, TT], F32, "lfs")
        for t in range(NT):
            xt = xrot.next()
            P.dma("sp", xt, xT, xt.ap[:], xT.ap[:, t * TT:(t + 1) * TT].rearrange("(k p) n -> p k n", p=128))
            stage_rms_hn(C, xt, TT, (prm, 0), hn, sq, rstd)
            hnf = lambda k: hn.ap[:, k, :]
            for blk, gcol, dst in ((0, KD, qT), (1, KD + 1, kT)):
                ob = obrot.next()
                for s in range(E // WSL):
                    w = load_wslab(C, wrot, w_in, blk * E + s * WSL, WSL, KD)
                    for j in range(WSL // 128):
                        h = s * (WSL // 128) + j
                        ps = C.nextps()
                        mm_group(C, ps, (0, TT), w, j * 128, hn, hnf, KD)
                        sqt = sqh.next()
                        P.op("act", lambda e, ps=ps, sqt=sqt: e.activation(out=sqt.ap[:], in_=ps.ap[:], func=AF.Square), reads=[ps], writes=[sqt])
                        ps2 = C.nextps()
                        P.op("pe", lambda e, ps2=ps2, sqt=sqt: e.matmul(ps2.ap[:], lhsT=C.ones.ap[:], rhs=sqt.ap[:], start=True, stop=True),
                             reads=[C.ones, sqt], writes=[ps2])
                        tmp = tmpr.next()
                        P.op("act", lambda e, ps2=ps2, tmp=tmp: e.activation(out=tmp.ap[:], in_=ps2.ap[:], func=AF.Ln, scale=1.0 / 128, bias=C.epsc.ap[:, 0:1]),
                             reads=[ps2, C.epsc], writes=[tmp])
                        P.op("act", lambda e, tmp=tmp: e.activation(out=tmp.ap[:], in_=tmp.ap[:], func=AF.Exp, scale=-0.5), reads=[tmp], writes=[tmp])
                        P.op("dve", lambda e, ps=ps, tmp=tmp, h=h, ob=ob, gcol=gcol: e.scalar_tensor_tensor(out=ob.ap[:, h, :], in0=ps.ap[:], scalar=prm.ap[:, gcol:gcol + 1],
                                                                                                 in1=tmp.ap[:], op0=ALU.mult, op1=ALU.mult),
                             reads=[ps, prm, tmp], writes=[ob])
                P.dma("sp", dst, ob, dst.ap[:, t * TT:(t + 1) * TT].rearrange("(h p) n -> p h n", p=128), ob.ap[:], semtile=ob, is_output=True)
            ob = obrot.next()
            for s in range(E // WSL):
                w = load_wslab(C, wrot, w_in, 3 * E + s * WSL, WSL, KD)
                for j in range(WSL // 128):
                    h = s * (WSL // 128) + j
                    ps = C.nextps()
                    mm_group(C, ps, (0, TT), w, j * 128, hn, hnf, KD)
                    P.op("act", lambda e, ps=ps, h=h, ob=ob: e.activation(out=ob.ap[:, h, :], in_=ps.ap[:], func=AF.Silu), reads=[ps], writes=[ob])
            P.dma("sp", szT, ob, szT.ap[:, t * TT:(t + 1) * TT].rearrange("(h p) n -> p h n", p=128), ob.ap[:], semtile=ob, is_output=True)
            for s in range(E // WSL):
                w = load_wslab(C, wrot, w_in, 2 * E + s * WSL, WSL, KD)
                for tb in range(TT // 128):
                    ps = C.nextps()
                    for k in range(KD):
                        P.op("pe", lambda e, k=k, ps=ps, w=w, tb=tb: e.matmul(ps.ap[:], lhsT=hn.ap[:, k, tb * 128:(tb + 1) * 128], rhs=w.ap[:, k, :],
                                                                         start=(k == 0), stop=(k == KD - 1)),
                             reads=[hn, w], writes=[ps])
                    if tb % 2 == 0:
                        P.op("act", lambda e, ps=ps, tb=tb, s=s: e.activation(out=vst.ap[:, tb, s * WSL:(s + 1) * WSL], in_=ps.ap[:], func=AF.Copy), reads=[ps], writes=[vst])
                    else:
                        P.op("dve", lambda e, ps=ps, tb=tb, s=s: e.tensor_copy(out=vst.ap[:, tb, s * WSL:(s + 1) * WSL], in_=ps.ap[:]), reads=[ps], writes=[vst])
            P.dma("sp", vO, vst, vO.ap[t * TT:(t + 1) * TT, :].rearrange("(tb p) c -> p tb c", p=128), vst.ap[:], semtile=vst, is_output=True)
            ps = C.nextps()
            for k in range(KD):
                P.op("pe", lambda e, k=k, ps=ps: e.matmul(ps.ap[0:HEADS, :], lhsT=wf.ap[:, k, :], rhs=hn.ap[:, k, :], start=(k == 0), stop=(k == KD - 1)),
                     reads=[wf, hn], writes=[ps])
            P.op("act", lambda e, ps=ps: e.activation(out=t16.ap[:], in_=ps.ap[0:HEADS, :], func=AF.Sigmoid, bias=fb.ap[:, 0:1]), reads=[ps, fb], writes=[t16])
            P.op("act", lambda e: e.activation(out=lfs.ap[:], in_=t16.ap[:], func=AF.Ln), reads=[t16], writes=[lfs])
            P.dma("sp", lfO, lfs, lfO.ap[:, t * TT:(t + 1) * TT], lfs.ap[:], semtile=lfs, is_output=True)
        P.emit()
    return nc


def build_b3():
    nc = bass.Bass("TRN2", target_bir_lowering=False)
    with ExitStack() as es:
        C = Ctx(nc, es)
        P = C.P
        xT = C.dram("xT", [D, TPC], F32, "ExternalInput")
        gT = C.dram("gT", [E, TPC], BF16, "ExternalInput")
        w_out = C.dram("w_out", [E, D], F32, "ExternalInput")
        yT = C.dram("yT", [D, TPC], F32, "ExternalOutput")
        xrot = Rot([C.sb([128, KD, TT], F32, f"x{i}") for i in range(2)])
        grot = Rot([C.sb([128, KE, TT], BF16, f"g{i}") for i in range(2)])
        worot = Rot([C.sb([128, KE, 256], BF16, f"wo{i}") for i in range(3)])
        for t in range(NT):
            xt = xrot.next()
            g = grot.next()
            P.dma("sp", xt, xT, xt.ap[:], xT.ap[:, t * TT:(t + 1) * TT].rearrange("(k p) n -> p k n", p=128))
            P.dma("sp", g, gT, g.ap[:], gT.ap[:, t * TT:(t + 1) * TT].rearrange("(k p) n -> p k n", p=128))
            stage_out_proj(C, w_out, worot, g, xt, xt)
            P.dma("sp", yT, xt, yT.ap[:, t * TT:(t + 1) * TT].rearrange("(k p) n -> p k n", p=128), xt.ap[:], semtile=xt, is_output=True)
        P.emit()
    return nc


HL = 4
NKB = SEQ // 128
NQT = SEQ // TT
DIAG = [(T_, kb_) for T_ in range(4) for kb_ in range(T_ + 1)]


def build_b2():
    nc = bass.Bass("TRN2", target_bir_lowering=False)
    SCALE = 128 ** -0.5
    with ExitStack() as es:
        C = Ctx(nc, es)
        P = C.P
        qT = C.dram("qT", [HL, 128, SEQ], BF16, "ExternalInput")
        kT = C.dram("kT", [HL, 128, SEQ], BF16, "ExternalInput")
        vD = C.dram("v", [HL, SEQ, 128], BF16, "ExternalInput")
        szT = C.dram("szT", [HL, 128, SEQ], BF16, "ExternalInput")
        lfR = C.dram("lfR", [HL, SEQ], F32, "ExternalInput")
        lfK = C.dram("lfK", [128, NKB * HL], F32, "ExternalInput")
        lf64_d = C.dram("lf64", [HL * NQT, TT], F32, "ExternalInput")
        scr_o = C.dram("scr_o", [HL * NQT, TT], F32, "Internal")
        scr_d = C.dram("scr_d", [HL * NQT, TT], F32, "Internal")
        cm_d = C.dram("cmask", [128, 128], F32, "ExternalInput")
        id_d = C.dram("ident", [128, 128], F32, "ExternalInput")
        tri_d = C.dram("tri", [128, 128], F32, "ExternalInput")
        oh_d = C.dram("oh", [HL, HL * 128], F32, "ExternalInput")
        gT = C.dram("gT", [HL, 128, SEQ], BF16, "ExternalOutput")

        stage_consts(C)
        cmask = C.sb([128, 128], BF16, "cmask")
        P.dma("pool", cmask, cm_d, cmask.ap[:], cm_d.ap[:])
        ident = C.sb([128, 128], BF16, "ident")
        P.dma("pool", ident, id_d, ident.ap[:], id_d.ap[:])
        tri = C.sb([128, 128], F32, "tri")
        P.dma("sp", tri, tri_d, tri.ap[:], tri_d.ap[:])
        oh = C.sb([HL, HL * 128], F32, "oh")
        P.dma("sp", oh, oh_d, oh.ap[:], oh_d.ap[:])
        lfk = C.sb([128, NKB * HL], F32, "lfk")
        P.dma("sp", lfk, lfK, lfk.ap[:], lfK.ap[:])
        zer = C.sb([HL * NQT, TT], F32, "zer")
        P.op("dve", lambda e: e.memset(zer.ap[:], 0.0), writes=[zer])

        lf64 = C.sb([HL * NQT, TT], F32, "lf64")
        P.dma("sp", lf64, lf64_d, lf64.ap[:], lf64_d.ap[:])
        cl512 = C.sb([HL * NQT, TT], F32, "cl512")
        cl128 = C.sb([HL * NQT, TT], F32, "cl128")
        P.op("dve", lambda e: e.tensor_tensor_scan(out=cl512.ap[:], data0=lf64.ap[:], data1=zer.ap[:, 0:TT], initial=0.0, op0=ALU.add, op1=ALU.add),
             reads=[lf64, zer], writes=[cl512])
        for i in range(4):
            P.op("dve", lambda e, i=i: e.tensor_tensor_scan(out=cl128.ap[:, i * 128:(i + 1) * 128], data0=lf64.ap[:, i * 128:(i + 1) * 128], data1=zer.ap[:, 0:128],
                                                            initial=0.0, op0=ALU.add, op1=ALU.add),
                 reads=[lf64, zer], writes=[cl128])
        P.op("act", lambda e: e.activation(out=cl512.ap[:], in_=cl512.ap[:], func=AF.Exp), reads=[cl512], writes=[cl512])
        P.op("act", lambda e: e.activation(out=cl128.ap[:], in_=cl128.ap[:], func=AF.Exp), reads=[cl128], writes=[cl128])
        P.dma("sp", scr_o, cl512, scr_o.ap[:], cl512.ap[:], semtile=cl512)
        P.dma("sp", scr_d, cl128, scr_d.ap[:], cl128.ap[:], semtile=cl128)
        bs = C.sb([HL, NKB], F32, "bs")
        lfc = C.sb([HL, SEQ // 4], F32, "lfc")
        for i in range(4):
            P.dma("sp", lfc, lfR, lfc.ap[:], lfR.ap[:, i * (SEQ // 4):(i + 1) * (SEQ // 4)])
            P.op("dve", lambda e, i=i: e.tensor_reduce(out=bs.ap[:, i * 16:(i + 1) * 16], in_=lfc.ap[:].rearrange("h (kb p) -> h kb p", p=128),
                                                       axis=mybir.AxisListType.X, op=ALU.add),
                 reads=[lfc], writes=[bs])
        G = C.sb([HL, NQT, NKB], F32, "G")
        P.op("dve", lambda e: e.memset(G.ap[:], 0.0), writes=[G])
        bsr = C.sb([HL, NKB], F32, "bsr")
        for i in range(NKB):
            P.op("dve", lambda e, i=i: e.tensor_copy(out=bsr.ap[:, i:i + 1], in_=bs.ap[:, NKB - 1 - i:NKB - i]), reads=[bs], writes=[bsr])
        Gr = C.sb([HL, NQT, NKB], F32, "Gr")
        P.op("dve", lambda e: e.memset(Gr.ap[:], 0.0), writes=[Gr])
        for tq in range(1, NQT):
            n = 4 * tq - 1
            if n <= 0:
                continue
            i0 = NKB - 1 - (4 * tq - 1)
            P.op("dve", lambda e, tq=tq, i0=i0, n=n: e.tensor_tensor_scan(out=Gr.ap[:, tq, i0 + 1:i0 + 1 + n], data0=bsr.ap[:, i0:i0 + n], data1=zer.ap[0:HL, 0:n],
                                                                      initial=0.0, op0=ALU.add, op1=ALU.add),
                 reads=[bsr, zer], writes=[Gr])
        for kb in range(NKB):
            P.op("dve", lambda e, kb=kb: e.tensor_copy(out=G.ap[:, :, kb:kb + 1], in_=Gr.ap[:, :, NKB - 1 - kb:NKB - kb]), reads=[Gr], writes=[G])
        Gd = C.sb([HL, NQT, 10], F32, "Gd")
        P.op("dve", lambda e: e.memset(Gd.ap[:], 0.0), writes=[Gd])
        bs4 = lambda i: bs.ap[:, i::4]
        gdv = lambda c: Gd.ap[:, :, c]
        ci = {tk: i for i, tk in enumerate(DIAG)}
        for T_ in range(4):
            P.op("dve", lambda e, T_=T_: e.tensor_scalar(out=gdv(ci[(T_, T_)]), in0=bs4(T_), scalar1=-1.0, scalar2=None, op0=ALU.mult), reads=[bs], writes=[Gd])
        P.op("dve", lambda e: e.tensor_copy(out=gdv(ci[(2, 0)]), in_=bs4(1)), reads=[bs], writes=[Gd])
        P.op("dve", lambda e: e.tensor_copy(out=gdv(ci[(3, 1)]), in_=bs4(2)), reads=[bs], writes=[Gd])
        P.op("dve", lambda e: e.tensor_tensor(out=gdv(ci[(3, 0)]), in0=bs4(1), in1=bs4(2), op=ALU.add), reads=[bs], writes=[Gd])
        Dloc = C.sb([128, NKB * HL], F32, "Dloc")
        psd = C.nextps()
        P.op("pe", lambda e: e.matmul(psd.ap[:, 0:NKB * HL], lhsT=tri.ap[:], rhs=lfk.ap[:], start=True, stop=True), reads=[tri, lfk], writes=[psd])
        P.op("act", lambda e: e.activation(out=Dloc.ap[:], in_=psd.ap[:, 0:NKB * HL], func=AF.Copy), reads=[psd], writes=[Dloc])
        btab = [C.sb([128, NQT, NKB], F32, f"btab{h}") for h in range(HL)]
        btd = [C.sb([128, NQT, 10], F32, f"btd{h}") for h in range(HL)]
        for h in range(HL):
            ohh = oh.ap[:, h * 128:(h + 1) * 128]
            for half in range(2):
                ps = C.nextps()
                P.op("pe", lambda e, ps=ps, half=half, ohh=ohh: e.matmul(ps.ap[:], lhsT=ohh, rhs=G.ap[:, half * 8:(half + 1) * 8, :], start=True, stop=True),
                     reads=[oh, G], writes=[ps])
                for tql in range(8):
                    tq = half * 8 + tql
                    P.op("dve", lambda e, ps=ps, tql=tql, tq=tq, h=h: e.tensor_tensor(out=btab[h].ap[:, tq, :], in0=ps.ap[:, tql * NKB:(tql + 1) * NKB],
                                                                                  in1=Dloc.ap[:, h::HL], op=ALU.add),
                         reads=[ps, Dloc], writes=[btab[h]])
            ps = C.nextps()
            P.op("pe", lambda e, ps=ps, ohh=ohh: e.matmul(ps.ap[:, 0:NQT * 10], lhsT=ohh, rhs=Gd.ap[:], start=True, stop=True), reads=[oh, Gd], writes=[ps])
            for (T_, kb_), c in ci.items():
                P.op("dve", lambda e, ps=ps, c=c, kb_=kb_, h=h: e.tensor_tensor(out=btd[h].ap[:, :, c], in0=ps.ap[:, c:NQT * 10:10],
                                                                              in1=Dloc.ap[:, kb_ * HL + h::4 * HL], op=ALU.add),
                     reads=[ps, Dloc], writes=[btd[h]])

        krot = Rot([C.sb([128, SEQ], BF16, f"k{i}") for i in range(2)])
        vrot = Rot([C.sb([128, NKB, 128], BF16, f"v{i}") for i in range(2)])
        qrot = Rot([C.sb([128, TT], BF16, f"q{i}") for i in range(3)])
        szrot = Rot([C.sb([128, TT], BF16, f"sz{i}") for i in range(3)])
        prot = Rot([C.sb([128, TT], BF16, f"p{i}") for i in range(4)])
        pdrot = Rot([C.sb([128, 128], BF16, f"pd{i}") for i in range(4)])
        worot = Rot([C.sb([128, TT], F32, f"wo{i}") for i in range(2)])
        wdrot = Rot([C.sb([128, TT], F32, f"wd{i}") for i in range(2)])
        numr = Rot([C.sb([128, TT], F32, f"num{i}") for i in range(2)])
        denr = Rot([C.sb([128, TT], F32, f"den{i}") for i in range(2)])
        n2r = Rot([C.sb([128, TT], F32, f"n2{i}") for i in range(2)])
        gor = Rot([C.sb([128, TT], BF16, f"go{i}") for i in range(2)])
        srot = Rot([C.ps[0], C.ps[1], C.ps[2]])
        A_off, l_off, A_d, l_d = C.ps[3], C.ps[4], C.ps[5], C.ps[6]

        for h in range(HL):
            kt = krot.next()
            vt = vrot.next()
            P.dma("sp", kt, kT, kt.ap[:], kT.ap[h])
            P.dma("sp", vt, vD, vt.ap[:], vD.ap[h].rearrange("(kb p) d -> p kb d", p=128))
            ohh = oh.ap[:, h * 128:(h + 1) * 128]
            for tq in range(NQT):
                qt = qrot.next()
                szt = szrot.next()
                P.dma("sp", qt, qT, qt.ap[:], qT.ap[h, :, tq * TT:(tq + 1) * TT])
                P.dma("sp", szt, szT, szt.ap[:], szT.ap[h, :, tq * TT:(tq + 1) * TT])
                nkb = 4 * tq
                for kb in range(nkb):
                    sp_ = srot.next()
                    P.op("pe", lambda e, sp_=sp_, kb=kb, kt=kt, qt=qt: e.matmul(sp_.ap[:], lhsT=kt.ap[:, kb * 128:(kb + 1) * 128], rhs=qt.ap[:], start=True, stop=True),
                         reads=[kt, qt], writes=[sp_])
                    pt = prot.next()
                    P.op("act", lambda e, sp_=sp_, pt=pt, kb=kb, tq=tq, h=h: e.activation(out=pt.ap[:], in_=sp_.ap[:], func=AF.Exp, scale=SCALE,
                                                                                   bias=btab[h].ap[:, tq, kb:kb + 1]),
                         reads=[sp_, btab[h]], writes=[pt])
                    P.op("pe", lambda e, pt=pt, kb=kb, vt=vt, nkb=nkb: e.matmul(A_off.ap[:], lhsT=vt.ap[:, kb, :], rhs=pt.ap[:], start=(kb == 0), stop=(kb == nkb - 1)),
                         reads=[vt, pt], writes=[A_off])
                    P.op("pe", lambda e, pt=pt, kb=kb, nkb=nkb: e.matmul(l_off.ap[:], lhsT=C.ones.ap[:], rhs=pt.ap[:], start=(kb == 0), stop=(kb == nkb - 1)),
                         reads=[C.ones, pt], writes=[l_off])
                for (T_, kb_), c in ci.items():
                    kba = 4 * tq + kb_
                    sp_ = srot.next()
                    last = (kb_ == T_)
                    P.op("pe", lambda e, sp_=sp_, kba=kba, T_=T_, kt=kt, qt=qt, last=last: e.matmul(sp_.ap[:, 0:128], lhsT=kt.ap[:, kba * 128:(kba + 1) * 128],
                                                                                             rhs=qt.ap[:, T_ * 128:(T_ + 1) * 128], start=True, stop=(not last)),
                         reads=[kt, qt], writes=[sp_])
                    if last:
                        P.op("pe", lambda e, sp_=sp_: e.matmul(sp_.ap[:, 0:128], lhsT=ident.ap[:], rhs=cmask.ap[:], start=False, stop=True),
                             reads=[ident, cmask], writes=[sp_])
                    pd = pdrot.next()
                    P.op("act", lambda e, sp_=sp_, pd=pd, c=c, tq=tq, h=h: e.activation(out=pd.ap[:], in_=sp_.ap[:, 0:128], func=AF.Exp, scale=SCALE,
                                                                                 bias=btd[h].ap[:, tq, c:c + 1]),
                         reads=[sp_, btd[h]], writes=[pd])
                    P.op("pe", lambda e, pd=pd, kba=kba, T_=T_, kb_=kb_, vt=vt, last=last: e.matmul(A_d.ap[:, T_ * 128:(T_ + 1) * 128], lhsT=vt.ap[:, kba, :], rhs=pd.ap[:],
                                                                                             start=(kb_ == 0), stop=last),
                         reads=[vt, pd], writes=[A_d])
                    P.op("pe", lambda e, pd=pd, T_=T_, kb_=kb_, last=last: e.matmul(l_d.ap[:, T_ * 128:(T_ + 1) * 128], lhsT=C.ones.ap[:], rhs=pd.ap[:],
                                                                                 start=(kb_ == 0), stop=last),
                         reads=[C.ones, pd], writes=[l_d])
                wd = wdrot.next()
                r = h * NQT + tq
                P.dma("sp", wd, scr_d, wd.ap[:], scr_d.ap[r, :].partition_broadcast(128))
                num, den = numr.next(), denr.next()
                P.op("dve", lambda e, num=num, wd=wd: e.tensor_tensor(out=num.ap[:], in0=A_d.ap[:], in1=wd.ap[:], op=ALU.mult), reads=[A_d, wd], writes=[num])
                P.op("dve", lambda e, den=den, wd=wd: e.tensor_tensor(out=den.ap[:], in0=l_d.ap[:], in1=wd.ap[:], op=ALU.mult), reads=[l_d, wd], writes=[den])
                if nkb > 0:
                    wo = worot.next()
                    P.dma("sp", wo, scr_o, wo.ap[:], scr_o.ap[r, :].partition_broadcast(128))
                    n2 = n2r.next()
                    P.op("dve", lambda e, n2=n2, wo=wo: e.tensor_tensor(out=n2.ap[:], in0=A_off.ap[:], in1=wo.ap[:], op=ALU.mult), reads=[A_off, wo], writes=[n2])
                    P.op("pool", lambda e, n2=n2, num=num: e.tensor_tensor(out=num.ap[:], in0=num.ap[:], in1=n2.ap[:], op=ALU.add), reads=[num, n2], writes=[num])
                    n3 = n2r.next()
                    P.op("dve", lambda e, n3=n3, wo=wo: e.tensor_tensor(out=n3.ap[:], in0=l_off.ap[:], in1=wo.ap[:], op=ALU.mult), reads=[l_off, wo], writes=[n3])
                    P.op("pool", lambda e, n3=n3, den=den: e.tensor_tensor(out=den.ap[:], in0=den.ap[:], in1=n3.ap[:], op=ALU.add), reads=[den, n3], writes=[den])
                P.op("dve", lambda e, den=den: e.reciprocal(out=den.ap[:], in_=den.ap[:]), reads=[den], writes=[den])
                P.op("dve", lambda e, num=num, den=den: e.tensor_tensor(out=num.ap[:], in0=num.ap[:], in1=den.ap[:], op=ALU.mult), reads=[num, den], writes=[num])
                go = gor.next()
                P.op("pool", lambda e, go=go, num=num, szt=szt: e.tensor_tensor(out=go.ap[:], in0=num.ap[:], in1=szt.ap[:], op=ALU.mult), reads=[num, szt], writes=[go])
                P.dma("sp", gT, go, gT.ap[h, :, tq * TT:(tq + 1) * TT], go.ap[:], semtile=go, is_output=True)
        P.emit()
    return nc


_CACHE = {}


def _prog(key, fn):
    if key not in _CACHE:
        _CACHE[key] = fn()
    return _CACHE[key]


def _run(nc, in_maps):
    res = run_bass_kernel_spmd(nc, in_maps, core_ids=list(range(NCORES)))
    return res.results


def _pack_cols(v, nchunks):
    return np.ascontiguousarray(v.reshape(nchunks, 128).T)


def run_conv_layer(kind, xT_full, norm_g, w_in, conv_w, w_out, conv_b=None, ln_g=None, ln_b=None):
    KW = conv_w.shape[0]
    HP = 32 if kind == "A" else 2
    cols = [_pack_cols(norm_g, KD)]
    cw = np.ascontiguousarray(conv_w.T).reshape(KE, 128, KW).transpose(1, 0, 2).reshape(128, KE * KW)
    cols.append(cw)
    z16 = np.zeros((128, KE), np.float32)
    cols.append(_pack_cols(conv_b, KE) if conv_b is not None else z16)
    cols.append(_pack_cols(ln_g, KE) if ln_g is not None else z16)
    cols.append(_pack_cols(ln_b, KE) if ln_b is not None else z16)
    prm = np.ascontiguousarray(np.concatenate(cols, axis=1).astype(np.float32))
    ident = np.eye(128, dtype=np.float32)
    nc = _prog(("conv", kind), lambda: build_conv_layer(kind))
    in_maps = []
    for c in range(NCORES):
        t0 = c * TPC
        xs = np.zeros((D, HP + TPC), np.float32)
        xs[:, HP:] = xT_full[:, t0:t0 + TPC]
        if t0 % SEQ != 0:
            xs[:, :HP] = xT_full[:, t0 - HP:t0]
        in_maps.append({"xT": xs, "w_in": np.ascontiguousarray(w_in), "w_out": np.ascontiguousarray(w_out), "prm": prm, "ident": ident})
    res = _run(nc, in_maps)
    return np.concatenate([r["yT"] for r in res], axis=1)


def run_attn_layer(xT_full, norm_g, w_in, f_bias, gq, gk, w_out, debug=None):
    prm = np.ascontiguousarray(np.concatenate([_pack_cols(norm_g, KD), gq.reshape(128, 1), gk.reshape(128, 1)], axis=1).astype(np.float32))
    fb = np.ascontiguousarray(f_bias.reshape(HEADS, 1).astype(np.float32))
    nc1 = _prog("b1", build_b1)
    w_in = np.ascontiguousarray(w_in)
    in_maps = [{"xT": np.ascontiguousarray(xT_full[:, c * TPC:(c + 1) * TPC]), "w_in": w_in, "prm": prm, "fb": fb} for c in range(NCORES)]
    res = _run(nc1, in_maps)
    qT = np.concatenate([r["qT"] for r in res], axis=1)
    kT = np.concatenate([r["kT"] for r in res], axis=1)
    szT = np.concatenate([r["szT"] for r in res], axis=1)
    v = np.concatenate([r["v"] for r in res], axis=0)
    lf = np.concatenate([r["lf"] for r in res], axis=1)
    if debug is not None:
        debug.update(qT=qT, kT=kT, szT=szT, v=v, lf=lf)
    nc2 = _prog("b2", build_b2)
    p_ = np.arange(128)
    cmask = np.where(p_[None, :] < p_[:, None], np.float32(-30000.0), np.float32(0.0)).astype(np.float32)
    ident = np.eye(128, dtype=np.float32)
    tri = (p_[:, None] > p_[None, :]).astype(np.float32)
    oh = np.zeros((HL, HL * 128), np.float32)
    for h in range(HL):
        oh[h, h * 128:(h + 1) * 128] = 1.0
    in_maps = []
    for c in range(NCORES):
        b, hq = c // 4, c % 4
        ts = slice(b * SEQ, (b + 1) * SEQ)
        rows = slice(hq * HL * 128, (hq + 1) * HL * 128)
        lfR = np.ascontiguousarray(lf[hq * HL:(hq + 1) * HL, ts])
        in_maps.append({
            "qT": np.ascontiguousarray(qT[rows, ts]).reshape(HL, 128, SEQ),
            "kT": np.ascontiguousarray(kT[rows, ts]).reshape(HL, 128, SEQ),
            "szT": np.ascontiguousarray(szT[rows, ts]).reshape(HL, 128, SEQ),
            "v": np.ascontiguousarray(v[ts, rows].reshape(SEQ, HL, 128).transpose(1, 0, 2)),
            "lfR": lfR,
            "lfK": np.ascontiguousarray(lfR.reshape(HL, NKB, 128).transpose(2, 1, 0).reshape(128, NKB * HL)),
            "lf64": np.ascontiguousarray(lfR.reshape(HL * NQT, TT)),
            "cmask": cmask, "ident": ident, "tri": tri, "oh": oh,
        })
    res = _run(nc2, in_maps)
    gT = np.zeros((E, NTOK), dtype=qT.dtype)
    for c in range(NCORES):
        b, hq = c // 4, c % 4
        gT[hq * HL * 128:(hq + 1) * HL * 128, b * SEQ:(b + 1) * SEQ] = res[c]["gT"].reshape(HL * 128, SEQ)
    if debug is not None:
        debug.update(gT=gT)
    nc3 = _prog("b3", build_b3)
    w_out = np.ascontiguousarray(w_out)
    in_maps = [{"xT": np.ascontiguousarray(xT_full[:, c * TPC:(c + 1) * TPC]), "gT": np.ascontiguousarray(gT[:, c * TPC:(c + 1) * TPC]), "w_out": w_out}
               for c in range(NCORES)]
    res = _run(nc3, in_maps)
    return np.concatenate([r["yT"] for r in res], axis=1)


def kernel(x, a_norm, a_w_in, a_conv_w, a_conv_b, a_ln_g, a_ln_b, a_w_out,
           b_norm, b_w_in, b_f_bias, b_q_norm, b_k_norm, b_w_out,
           c_norm, c_w_in, c_conv_w, c_w_out):
    f = lambda a: np.asarray(a, dtype=np.float32)
    x = f(x)
    xT = np.ascontiguousarray(x.reshape(NTOK, D).T)
    xT = run_conv_layer("A", xT, f(a_norm)[0], f(a_w_in)[0], f(a_conv_w)[0], f(a_w_out)[0], f(a_conv_b)[0], f(a_ln_g)[0], f(a_ln_b)[0])
    xT = run_attn_layer(xT, f(b_norm)[0], f(b_w_in)[0], f(b_f_bias)[0], f(b_q_norm)[0], f(b_k_norm)[0], f(b_w_out)[0])
    xT = run_conv_layer("C", xT, f(c_norm)[0], f(c_w_in)[0], f(c_conv_w)[0], f(c_w_out)[0])
    xT = run_conv_layer("A", xT, f(a_norm)[1], f(a_w_in)[1], f(a_conv_w)[1], f(a_w_out)[1], f(a_conv_b)[1], f(a_ln_g)[1], f(a_ln_b)[1])
    return np.ascontiguousarray(xT.T).reshape(BATCH, SEQ, D)
```

```python
import numpy as np
from contextlib import ExitStack
import concourse.bass as bass
import concourse.mybir as mybir
from concourse.bass_utils import run_bass_kernel_spmd

F32 = mybir.dt.float32
BF16 = mybir.dt.bfloat16
AF = mybir.ActivationFunctionType
ALU = mybir.AluOpType

NCORES = 8
D = 1024
E = 2048
SEQ = 8192
BATCH = 2
NTOK = BATCH * SEQ
TPC = NTOK // NCORES
TT = 512
NT = TPC // TT
KD = D // 128
KE = E // 128
EPS = 1e-6
HEADS = 16

SAME_ENGINE_RAW_SYNC = True


class T:
    __slots__ = ("ap", "name", "w", "r", "dsem")

    def __init__(self, ap, name=""):
        self.ap = ap
        self.name = name
        self.w = None
        self.r = []
        self.dsem = None

    def reset(self):
        self.w = None
        self.r = []
        self.dsem = None


class Op:
    __slots__ = ("fn", "deps", "signal", "dma", "sigval")

    def __init__(self, fn, deps, dma):
        self.fn = fn
        self.deps = deps
        self.signal = False
        self.dma = dma
        self.sigval = None


class Prog:
    ENGS = ("pe", "act", "dve", "pool", "sp")

    def __init__(self, nc, pool):
        self.nc = nc
        self.pool = pool
        self.used = []
        self.ops = {e: [] for e in self.ENGS}
        self.dma_latest = {}

    def _alloc(self):
        if self.pool:
            rec = self.pool.pop()
        else:
            rec = [self.nc.alloc_semaphore(name=f"sm{Prog._nsem}"), 0]
            Prog._nsem += 1
        self.used.append(rec)
        return rec

    def _deps(self, eng, reads, writes):
        deps = []
        for t in reads:
            if t.w is not None:
                deps.append((t.w, "raw"))
        for t in writes:
            if t.w is not None:
                deps.append((t.w, "waw"))
            for ev in t.r:
                deps.append((ev, "war"))
        out = set()
        for ev, kind in deps:
            if ev[0] == "eng" and ev[1] == eng:
                if kind == "raw" and SAME_ENGINE_RAW_SYNC and eng != "pe":
                    out.add(ev)
                continue
            out.add(ev)
        return out

    def _update(self, ev, reads, writes):
        for t in reads:
            if ev[0] == "eng":
                t.r = [e for e in t.r if not (e[0] == "eng" and e[1] == ev[1])]
            t.r.append(ev)
        for t in writes:
            t.w = ev
            t.r = []

    def op(self, eng, fn, reads=(), writes=()):
        deps = self._deps(eng, reads, writes)
        self.ops[eng].append(Op(fn, deps, None))
        ev = ("eng", eng, len(self.ops[eng]) - 1)
        self._update(ev, reads, writes)
        return ev

    def dma(self, eng, out_t, in_t, out_ap, in_ap, semtile=None, is_output=False, **kw):
        st = semtile if semtile is not None else out_t
        if st.dsem is None:
            st.dsem = self._alloc()
        st.dsem[1] += 16
        sem, val = st.dsem[0], st.dsem[1]
        deps = self._deps(eng, [in_t], [out_t])
        fn = lambda e, out_ap=out_ap, in_ap=in_ap, kw=kw: e.dma_start(out=out_ap, in_=in_ap, **kw)
        self.ops[eng].append(Op(fn, deps, (sem, 16)))
        ev = ("dma", sem, val)
        self.dma_latest[id(sem)] = ev
        self._update(ev, [in_t], [out_t])
        return ev

    def coll(self, kind, out_t, in_t, out_ap, in_ap):
        rec = self._alloc()
        rec[1] += 1
        sem, val = rec[0], rec[1]
        deps = self._deps("pool", [in_t], [out_t])
        fn = lambda e: e.collective_compute(kind, ALU.bypass, replica_groups=[list(range(NCORES))], ins=[in_ap], outs=[out_ap])
        self.ops["pool"].append(Op(fn, deps, (sem, None)))
        ev = ("dma", sem, val)
        self.dma_latest[id(sem)] = ev
        self._update(ev, [in_t], [out_t])
        return ev

    def emit(self):
        nc = self.nc
        for e in self.ENGS:
            for o in self.ops[e]:
                for d in o.deps:
                    if d[0] == "eng":
                        self.ops[d[1]][d[2]].signal = True
        esem = {}
        for e in self.ENGS:
            rec = self._alloc()
            esem[e] = rec
            c = rec[1]
            for o in self.ops[e]:
                if o.signal:
                    c += 1
                    o.sigval = c
            rec[1] = c
        ops = self.ops
        final = list(self.dma_latest.values())

        def run(ename, eng):
            waited = {}
            for o in ops[ename]:
                for d in o.deps:
                    if d[0] == "eng":
                        sem = esem[d[1]][0]
                        val = ops[d[1]][d[2]].sigval
                        key = ("e", d[1])
                    else:
                        sem, val = d[1], d[2]
                        key = ("d", id(sem))
                    if waited.get(key, 0) >= val:
                        continue
                    waited[key] = val
                    eng.wait_ge(sem, val)
                inst = o.fn(eng)
                if o.dma is not None:
                    if o.dma[1] is None:
                        inst.then_inc(o.dma[0])
                    else:
                        inst.then_inc(o.dma[0], o.dma[1])
                elif o.signal:
                    inst.then_inc(esem[ename][0], 1)
            if ename == "sp":
                for d in final:
                    eng.wait_ge(d[1], d[2])

        with nc.Block() as block:
            @block.tensor
            def _(eng):
                run("pe", eng)

            @block.scalar
            def _(eng):
                run("act", eng)

            @block.vector
            def _(eng):
                run("dve", eng)

            @block.gpsimd
            def _(eng):
                run("pool", eng)

            @block.sync
            def _(eng):
                run("sp", eng)
        self.pool.extend(self.used)


Prog._nsem = 0


class Ctx:
    def __init__(self, nc, es):
        self.nc = nc
        self.es_outer = es
        self.es = es
        self.pool = []
        self.persist = []
        self.P = None
        self.nm = 0
        self.ps = [self.psum(f"psb{i}") for i in range(8)]
        self.psi = 0

    def sb(self, shape, dt, name=None, persistent=False):
        self.nm += 1
        name = "sb_" + (name or "t") + f"_{self.nm}"
        es = self.es_outer if persistent else self.es
        ap = es.enter_context(self.nc.sbuf_tensor(name, list(shape), dt))
        t = T(ap, name)
        if persistent:
            self.persist.append(t)
        return t

    def psum(self, name):
        ap = self.es_outer.enter_context(self.nc.psum_tensor(name, [128, 512], F32))
        t = T(ap, name)
        self.persist.append(t)
        return t

    def dram(self, name, shape, dt, kind=None):
        if kind is None:
            t = T(self.nc.dram_tensor(name, list(shape), dt).ap(), name)
        else:
            t = T(self.nc.dram_tensor(name, list(shape), dt, kind=kind).ap(), name)
        self.persist.append(t)
        return t

    def nextps(self, nrot=6):
        t = self.ps[self.psi % nrot]
        self.psi += 1
        return t

    def begin(self):
        self.es = ExitStack()
        self.P = Prog(self.nc, self.pool)

    def end(self):
        self.P.emit()
        for t in self.persist:
            t.reset()
        self.es.close()
        self.es = self.es_outer
        self.P = None


class Rot:
    def __init__(self, tiles):
        self.tiles = tiles
        self.i = 0

    def next(self):
        t = self.tiles[self.i % len(self.tiles)]
        self.i += 1
        return t


def stage_consts(C):
    P = C.P
    C.ones = C.sb([128, 128], BF16, "ones", persistent=True)
    P.op("dve", lambda e: e.memset(C.ones.ap[:], 1.0), writes=[C.ones])
    C.epsc = C.sb([128, 1], F32, "epsc", persistent=True)
    P.op("dve", lambda e: e.memset(C.epsc.ap[:], EPS), writes=[C.epsc])
    C.ident = C.sb([128, 128], BF16, "ident", persistent=True)
    P.dma("pool", C.ident, C.id_d, C.ident.ap[:], C.id_d.ap[:])


def stage_rms_hn(C, xt, ncols, gcol, hn, sq, rstd):
    P = C.P
    ps = C.nextps()
    for k in range(KD):
        P.op("act", lambda e, k=k: e.activation(out=sq.ap[:, k, 0:ncols], in_=xt.ap[:, k, 0:ncols], func=AF.Square),
             reads=[xt], writes=[sq])
    for k in range(KD):
        P.op("pe", lambda e, k=k: e.matmul(ps.ap[:, 0:ncols], lhsT=C.ones.ap[:], rhs=sq.ap[:, k, 0:ncols],
                                           start=(k == 0), stop=(k == KD - 1)),
             reads=[C.ones, sq], writes=[ps])
    P.op("act", lambda e: e.activation(out=rstd.ap[:, 0:ncols], in_=ps.ap[:, 0:ncols], func=AF.Ln, scale=1.0 / D, bias=C.epsc.ap[:, 0:1]),
         reads=[ps, C.epsc], writes=[rstd])
    P.op("act", lambda e: e.activation(out=rstd.ap[:, 0:ncols], in_=rstd.ap[:, 0:ncols], func=AF.Exp, scale=-0.5),
         reads=[rstd], writes=[rstd])
    gt, goff = gcol
    for k in range(KD):
        P.op("dve", lambda e, k=k: e.scalar_tensor_tensor(out=hn.ap[:, k, 0:ncols], in0=xt.ap[:, k, 0:ncols],
                                                         scalar=gt.ap[:, goff + k:goff + k + 1], in1=rstd.ap[:, 0:ncols],
                                                         op0=ALU.mult, op1=ALU.mult),
             reads=[xt, gt, rstd], writes=[hn])


def load_wslab(C, wrot, wdram, col0, ncol, kchunks):
    wt = wrot.next()
    src = wdram.ap[:, col0:col0 + ncol].rearrange("(k p) n -> p k n", p=128)
    C.P.dma("pool", wt, wdram, wt.ap[:, 0:kchunks, 0:ncol], src)
    return wt


def mm_group(C, ps, pscols, wt, wcol0, rhs_t, rhs_fn, kchunks):
    for k in range(kchunks):
        C.P.op("pe", lambda e, k=k: e.matmul(ps.ap[:, pscols[0]:pscols[1]], lhsT=wt.ap[:, k, wcol0:wcol0 + 128], rhs=rhs_fn(k),
                                             start=(k == 0), stop=(k == kchunks - 1)),
               reads=[wt, rhs_t], writes=[ps])


def stage_out_proj(C, wout_dram, wrot, g, xt, xo, ncols=TT):
    P = C.P
    SL = 256
    for s in range(D // SL):
        wt = load_wslab(C, wrot, wout_dram, s * SL, SL, KE)
        for j in range(SL // 128):
            n = s * (SL // 128) + j
            ps = C.nextps()
            mm_group(C, ps, (0, ncols), wt, j * 128, g, lambda k: g.ap[:, k, 0:ncols], KE)
            P.op("dve", lambda e, n=n, ps=ps: e.tensor_tensor(out=xo.ap[:, n, 0:ncols], in0=ps.ap[:, 0:ncols], in1=xt.ap[:, n, 0:ncols], op=ALU.add),
                 reads=[ps, xt], writes=[xo])


HL = 4
NKB = SEQ // 128
NQT = SEQ // TT
DIAG = [(T_, kb_) for T_ in range(4) for kb_ in range(T_ + 1)]


def phase_conv(C, kind, xT, yT, halo_fn, w_in, w_out, prm_d):
    KW = 31 if kind == "A" else 3
    H = KW - 1
    HP = 32 if kind == "A" else 2
    C.begin()
    P = C.P
    O_G, O_CW = 0, KD
    O_CB = O_CW + KE * KW
    O_LG = O_CB + KE
    O_LB = O_LG + KE
    NP = O_LB + KE
    ident = C.ident
    prm = C.sb([128, NP], F32, "prm")
    P.dma("sp", prm, prm_d, prm.ap[:], prm_d.ap[:])
    xrot = Rot([C.sb([128, KD, TT], F32, f"x{i}") for i in range(2)])
    hn = C.sb([128, KD, TT], BF16, "hn")
    sq = C.sb([128, KD, TT], BF16, "sq")
    rstd = C.sb([128, TT], F32, "rstd")
    xh = C.sb([128, KD, HP], F32, "xh")
    hnh = C.sb([128, KD, HP], BF16, "hnh")
    sqh = C.sb([128, KD, HP], BF16, "sqh")
    rstdh = C.sb([128, HP], F32, "rstdh")
    urot = Rot([C.sb([128, KE, HP + TT], BF16, f"u{i}") for i in range(1)])
    sz = C.sb([128, KE, TT], BF16, "sz")
    g = C.sb([128, KE, TT], BF16, "g")
    tmpr = Rot([C.sb([128, TT], F32, f"tmp{i}") for i in range(4)])
    tmph = C.sb([128, HP], F32, "tmph")
    WSL = 512
    wrot = Rot([C.sb([128, KD, WSL], BF16, f"w{i}") for i in range(4)])
    worot = Rot([C.sb([128, KE, 256], BF16, f"wo{i}") for i in range(2)])
    dgrot = Rot([C.sb([128, KW, 128], BF16, f"dg{i}") for i in range(2)])
    if kind == "A":
        vb = C.sb([128, KE, TT], BF16, "vb")
        vsq = g
        mu = C.sb([128, TT], F32, "mu")
        lr = C.sb([128, TT], F32, "lr")
        psS, psQ = C.ps[6], C.ps[7]
    else:
        bsb = C.sb([128, KE, TT], BF16, "bsb")

    blkA, blkB = (0, 1) if kind == "A" else (0, 2)

    def evac_pair(psa, psb, cols, tmp_ap, out_ap, reads_extra=()):
        fn = AF.Sigmoid if kind == "A" else AF.Copy
        return fn

    ucur = urot.next()
    halo_fn(xh)
    stage_rms_hn(C, xh, HP, (prm, O_G), hnh, sqh, rstdh)
    for s in range(E // WSL):
        wa = load_wslab(C, wrot, w_in, blkA * E + s * WSL, WSL, KD)
        wb = load_wslab(C, wrot, w_in, blkB * E + s * WSL, WSL, KD)
        for j in range(WSL // 128):
            ch = s * (WSL // 128) + j
            psa, psb = C.nextps(), C.nextps()
            mm_group(C, psa, (0, HP), wa, j * 128, hnh, lambda k: hnh.ap[:, k, 0:HP], KD)
            mm_group(C, psb, (0, HP), wb, j * 128, hnh, lambda k: hnh.ap[:, k, 0:HP], KD)
            if kind == "A":
                P.op("act", lambda e, psb=psb: e.activation(out=tmph.ap[:], in_=psb.ap[:, 0:HP], func=AF.Sigmoid), reads=[psb], writes=[tmph])
            else:
                P.op("act", lambda e, psb=psb: e.activation(out=tmph.ap[:], in_=psb.ap[:, 0:HP], func=AF.Copy), reads=[psb], writes=[tmph])
            P.op("dve", lambda e, psa=psa, ch=ch, ucur=ucur: e.tensor_tensor(out=ucur.ap[:, ch, 0:HP], in0=psa.ap[:, 0:HP], in1=tmph.ap[:], op=ALU.mult),
                 reads=[psa, tmph], writes=[ucur])

    xnext = xrot.next()
    P.dma("sp", xnext, xT, xnext.ap[:], xT.ap[:, 0:TT].rearrange("(k p) n -> p k n", p=128))
    stage_rms_hn(C, xnext, TT, (prm, O_G), hn, sq, rstd)
    for t in range(NT):
        xt = xnext
        xo = xt
        if t + 1 < NT:
            xnext = xrot.next()
            P.dma("sp", xnext, xT, xnext.ap[:], xT.ap[:, (t + 1) * TT:(t + 2) * TT].rearrange("(k p) n -> p k n", p=128))
        hnf = lambda k: hn.ap[:, k, :]
        for s in range(E // WSL):
            wa = load_wslab(C, wrot, w_in, blkA * E + s * WSL, WSL, KD)
            wb = load_wslab(C, wrot, w_in, blkB * E + s * WSL, WSL, KD)
            for j in range(WSL // 128):
                ch = s * (WSL // 128) + j
                psa, psb = C.nextps(), C.nextps()
                mm_group(C, psa, (0, TT), wa, j * 128, hn, hnf, KD)
                mm_group(C, psb, (0, TT), wb, j * 128, hn, hnf, KD)
                tmp = tmpr.next()
                fn = AF.Sigmoid if kind == "A" else AF.Copy
                P.op("act", lambda e, psb=psb, tmp=tmp, fn=fn: e.activation(out=tmp.ap[:], in_=psb.ap[:], func=fn), reads=[psb], writes=[tmp])
                P.op("dve", lambda e, psa=psa, tmp=tmp, ch=ch, ucur=ucur: e.tensor_tensor(out=ucur.ap[:, ch, HP:HP + TT], in0=psa.ap[:], in1=tmp.ap[:], op=ALU.mult),
                     reads=[psa, tmp], writes=[ucur])
        zblk = 2 if kind == "A" else 3
        for s in range(E // WSL if kind == "C" else 0):
            wz = load_wslab(C, wrot, w_in, zblk * E + s * WSL, WSL, KD)
            wbg = load_wslab(C, wrot, w_in, 1 * E + s * WSL, WSL, KD) if kind == "C" else None
            for j in range(WSL // 128):
                ch = s * (WSL // 128) + j
                psz = C.nextps()
                mm_group(C, psz, (0, TT), wz, j * 128, hn, hnf, KD)
                P.op("act", lambda e, psz=psz, ch=ch: e.activation(out=sz.ap[:, ch, :], in_=psz.ap[:], func=AF.Silu), reads=[psz], writes=[sz])
                if kind == "C":
                    psg = C.nextps()
                    mm_group(C, psg, (0, TT), wbg, j * 128, hn, hnf, KD)
                    P.op("act", lambda e, psg=psg, ch=ch: e.activation(out=bsb.ap[:, ch, :], in_=psg.ap[:], func=AF.Copy), reads=[psg], writes=[bsb])
        pend_stats = []
        for ch in range(KE):
            dg = dgrot.next()
            for k in range(KW):
                P.op("dve", lambda e, k=k, ch=ch, dg=dg: e.tensor_scalar(out=dg.ap[:, k, :], in0=ident.ap[:], scalar1=prm.ap[:, O_CW + ch * KW + k:O_CW + ch * KW + k + 1],
                                                                   scalar2=None, op0=ALU.mult),
                     reads=[ident, prm], writes=[dg])
            psc = C.nextps()
            for k in range(KW):
                off = HP - H + k
                P.op("pe", lambda e, k=k, ch=ch, dg=dg, psc=psc, off=off, ucur=ucur: e.matmul(psc.ap[:], lhsT=dg.ap[:, k, :], rhs=ucur.ap[:, ch, off:off + TT],
                                                                                             start=(k == 0), stop=(k == KW - 1)),
                     reads=[dg, ucur], writes=[psc])
            if kind == "A":
                P.op("act", lambda e, ch=ch, psc=psc: e.activation(out=vb.ap[:, ch, :], in_=psc.ap[:], func=AF.Identity, bias=prm.ap[:, O_CB + ch:O_CB + ch + 1]),
                     reads=[psc, prm], writes=[vb])
                P.op("act", lambda e, ch=ch, psc=psc: e.activation(out=vsq.ap[:, ch, :], in_=psc.ap[:], func=AF.Square, bias=prm.ap[:, O_CB + ch:O_CB + ch + 1]),
                     reads=[psc, prm], writes=[vsq])
                def stats(ch=ch):
                    P.op("pe", lambda e: e.matmul(psS.ap[:], lhsT=C.ones.ap[:], rhs=vb.ap[:, ch, :], start=(ch == 0), stop=(ch == KE - 1)),
                         reads=[C.ones, vb], writes=[psS])
                    P.op("pe", lambda e: e.matmul(psQ.ap[:], lhsT=C.ones.ap[:], rhs=vsq.ap[:, ch, :], start=(ch == 0), stop=(ch == KE - 1)),
                         reads=[C.ones, vsq], writes=[psQ])
                if pend_stats:
                    pend_stats.pop()()
                pend_stats.append(stats)
            else:
                tmp = tmpr.next()
                P.op("dve", lambda e, ch=ch, psc=psc, tmp=tmp: e.tensor_tensor(out=tmp.ap[:], in0=psc.ap[:], in1=bsb.ap[:, ch, :], op=ALU.mult),
                     reads=[psc, bsb], writes=[tmp])
                P.op("dve", lambda e, ch=ch, tmp=tmp: e.tensor_tensor(out=g.ap[:, ch, :], in0=tmp.ap[:], in1=sz.ap[:, ch, :], op=ALU.mult),
                     reads=[tmp, sz], writes=[g])
        if kind == "A":
            pend_stats.pop()()
            P.op("act", lambda e: e.activation(out=mu.ap[:], in_=psS.ap[:], func=AF.Copy, scale=1.0 / E), reads=[psS], writes=[mu])
            tmp = tmpr.next()
            P.op("dve", lambda e, tmp=tmp: e.tensor_tensor(out=tmp.ap[:], in0=mu.ap[:], in1=mu.ap[:], op=ALU.mult), reads=[mu], writes=[tmp])
            P.op("dve", lambda e, tmp=tmp: e.scalar_tensor_tensor(out=lr.ap[:], in0=psQ.ap[:], scalar=1.0 / E, in1=tmp.ap[:], op0=ALU.mult, op1=ALU.subtract),
                 reads=[psQ, tmp], writes=[lr])
            P.op("act", lambda e: e.activation(out=lr.ap[:], in_=lr.ap[:], func=AF.Ln, bias=C.epsc.ap[:, 0:1]), reads=[lr, C.epsc], writes=[lr])
            P.op("act", lambda e: e.activation(out=lr.ap[:], in_=lr.ap[:], func=AF.Exp, scale=-0.5), reads=[lr], writes=[lr])
            for ch in range(KE):
                if ch % (WSL // 128) == 0:
                    wz = load_wslab(C, wrot, w_in, zblk * E + (ch // (WSL // 128)) * WSL, WSL, KD)
                psz = C.nextps()
                mm_group(C, psz, (0, TT), wz, (ch % (WSL // 128)) * 128, hn, hnf, KD)
                P.op("act", lambda e, psz=psz, ch=ch: e.activation(out=sz.ap[:, ch, :], in_=psz.ap[:], func=AF.Silu), reads=[psz], writes=[sz])
                t1, t2 = tmpr.next(), tmpr.next()
                P.op("dve", lambda e, ch=ch, t1=t1: e.tensor_tensor(out=t1.ap[:], in0=vb.ap[:, ch, :], in1=mu.ap[:], op=ALU.subtract), reads=[vb, mu], writes=[t1])
                P.op("dve", lambda e, t1=t1: e.tensor_tensor(out=t1.ap[:], in0=t1.ap[:], in1=lr.ap[:], op=ALU.mult), reads=[t1, lr], writes=[t1])
                P.op("act", lambda e, ch=ch, t1=t1, t2=t2: e.activation(out=t2.ap[:], in_=t1.ap[:], func=AF.Silu, scale=prm.ap[:, O_LG + ch:O_LG + ch + 1],
                                                                   bias=prm.ap[:, O_LB + ch:O_LB + ch + 1]),
                     reads=[t1, prm], writes=[t2])
                P.op("dve", lambda e, ch=ch, t2=t2: e.tensor_tensor(out=g.ap[:, ch, :], in0=t2.ap[:], in1=sz.ap[:, ch, :], op=ALU.mult),
                     reads=[t2, sz], writes=[g])
        unext = urot.next()
        if t < NT - 1:
            P.op("dve", lambda e, ucur=ucur, unext=unext: e.tensor_copy(out=unext.ap[:, :, 0:HP], in_=ucur.ap[:, :, TT:TT + HP]), reads=[ucur], writes=[unext])
        ucur = unext
        if t + 1 < NT:
            stage_rms_hn(C, xnext, TT, (prm, O_G), hn, sq, rstd)
        stage_out_proj(C, w_out, worot, g, xt, xo)
        P.dma("sp", yT, xo, yT.ap[:, t * TT:(t + 1) * TT].rearrange("(k p) n -> p k n", p=128), xo.ap[:], semtile=xo)
    C.end()


def phase_b2(C, qT, kT, vD, szT, lfR, g_loc, g_all, cm_d, tri_d, oh_d, scr_o, scr_d):
    SCALE = 128 ** -0.5
    C.begin()
    P = C.P
    cmask = C.sb([128, 128], BF16, "cmask")
    P.dma("pool", cmask, cm_d, cmask.ap[:], cm_d.ap[:])
    ident = C.ident
    tri = C.sb([128, 128], F32, "tri")
    P.dma("sp", tri, tri_d, tri.ap[:], tri_d.ap[:])
    oh = C.sb([HL, HL * 128], F32, "oh")
    P.dma("sp", oh, oh_d, oh.ap[:], oh_d.ap[:])
    lfk = C.sb([128, NKB * HL], F32, "lfk")
    lfr = C.sb([HL, SEQ], F32, "lfr")
    P.dma("sp", lfr, lfR, lfr.ap[:], lfR.ap[:])
    for half in range(2):
        pst = C.nextps()
        for kbl in range(NKB // 2):
            kb = half * (NKB // 2) + kbl
            P.op("pe", lambda e, pst=pst, kb=kb, kbl=kbl: e.matmul(pst.ap[:, kbl * HL:(kbl + 1) * HL], lhsT=lfr.ap[:, kb * 128:(kb + 1) * 128], rhs=oh.ap[:, 0::128], start=True, stop=True),
                 reads=[lfr, oh], writes=[pst])
        P.op("act", lambda e, pst=pst, half=half: e.activation(out=lfk.ap[:, half * (NKB // 2) * HL:(half + 1) * (NKB // 2) * HL], in_=pst.ap[:, 0:(NKB // 2) * HL], func=AF.Copy), reads=[pst], writes=[lfk])
    zer = C.sb([HL * NQT, TT], F32, "zer")
    P.op("dve", lambda e: e.memset(zer.ap[:], 0.0), writes=[zer])

    lf64 = C.sb([HL * NQT, TT], F32, "lf64")
    P.dma("sp", lf64, lfR, lf64.ap[:], lfR.ap.rearrange("h (t c) -> (h t) c", c=TT))
    cl512 = C.sb([HL * NQT, TT], F32, "cl512")
    cl128 = C.sb([HL * NQT, TT], F32, "cl128")
    P.op("dve", lambda e: e.tensor_tensor_scan(out=cl512.ap[:], data0=lf64.ap[:], data1=zer.ap[:, 0:TT], initial=0.0, op0=ALU.add, op1=ALU.add),
         reads=[lf64, zer], writes=[cl512])
    for i in range(4):
        P.op("dve", lambda e, i=i: e.tensor_tensor_scan(out=cl128.ap[:, i * 128:(i + 1) * 128], data0=lf64.ap[:, i * 128:(i + 1) * 128], data1=zer.ap[:, 0:128],
                                                        initial=0.0, op0=ALU.add, op1=ALU.add),
             reads=[lf64, zer], writes=[cl128])
    P.op("act", lambda e: e.activation(out=cl512.ap[:], in_=cl512.ap[:], func=AF.Exp), reads=[cl512], writes=[cl512])
    P.op("act", lambda e: e.activation(out=cl128.ap[:], in_=cl128.ap[:], func=AF.Exp), reads=[cl128], writes=[cl128])
    P.dma("sp", scr_o, cl512, scr_o.ap[:], cl512.ap[:], semtile=cl512)
    P.dma("sp", scr_d, cl128, scr_d.ap[:], cl128.ap[:], semtile=cl128)
    bs = C.sb([HL, NKB], F32, "bs")
    lfc = C.sb([HL, SEQ // 4], F32, "lfc")
    for i in range(4):
        P.op("act", lambda e, i=i: e.activation(out=lfc.ap[:], in_=lfr.ap[:, i * (SEQ // 4):(i + 1) * (SEQ // 4)], func=AF.Copy), reads=[lfr], writes=[lfc])
        P.op("dve", lambda e, i=i: e.tensor_reduce(out=bs.ap[:, i * 16:(i + 1) * 16], in_=lfc.ap[:].rearrange("h (kb p) -> h kb p", p=128),
                                                   axis=mybir.AxisListType.X, op=ALU.add),
             reads=[lfc], writes=[bs])
    G = C.sb([HL, NQT, NKB], F32, "G")
    P.op("dve", lambda e: e.memset(G.ap[:], 0.0), writes=[G])
    bsr = C.sb([HL, NKB], F32, "bsr")
    for i in range(NKB):
        P.op("dve", lambda e, i=i: e.tensor_copy(out=bsr.ap[:, i:i + 1], in_=bs.ap[:, NKB - 1 - i:NKB - i]), reads=[bs], writes=[bsr])
    Gr = C.sb([HL, NQT, NKB], F32, "Gr")
    P.op("dve", lambda e: e.memset(Gr.ap[:], 0.0), writes=[Gr])
    for tq in range(1, NQT):
        n = 4 * tq - 1
        if n <= 0:
            continue
        i0 = NKB - 1 - (4 * tq - 1)
        P.op("dve", lambda e, tq=tq, i0=i0, n=n: e.tensor_tensor_scan(out=Gr.ap[:, tq, i0 + 1:i0 + 1 + n], data0=bsr.ap[:, i0:i0 + n], data1=zer.ap[0:HL, 0:n],
                                                                  initial=0.0, op0=ALU.add, op1=ALU.add),
             reads=[bsr, zer], writes=[Gr])
    for kb in range(NKB):
        P.op("dve", lambda e, kb=kb: e.tensor_copy(out=G.ap[:, :, kb:kb + 1], in_=Gr.ap[:, :, NKB - 1 - kb:NKB - kb]), reads=[Gr], writes=[G])
    Gd = C.sb([HL, NQT, 10], F32, "Gd")
    P.op("dve", lambda e: e.memset(Gd.ap[:], 0.0), writes=[Gd])
    bs4 = lambda i: bs.ap[:, i::4]
    gdv = lambda c: Gd.ap[:, :, c]
    ci = {tk: i for i, tk in enumerate(DIAG)}
    for T_ in range(4):
        P.op("dve", lambda e, T_=T_: e.tensor_scalar(out=gdv(ci[(T_, T_)]), in0=bs4(T_), scalar1=-1.0, scalar2=None, op0=ALU.mult), reads=[bs], writes=[Gd])
    P.op("dve", lambda e: e.tensor_copy(out=gdv(ci[(2, 0)]), in_=bs4(1)), reads=[bs], writes=[Gd])
    P.op("dve", lambda e: e.tensor_copy(out=gdv(ci[(3, 1)]), in_=bs4(2)), reads=[bs], writes=[Gd])
    P.op("dve", lambda e: e.tensor_tensor(out=gdv(ci[(3, 0)]), in0=bs4(1), in1=bs4(2), op=ALU.add), reads=[bs], writes=[Gd])
    Dloc = C.sb([128, NKB * HL], F32, "Dloc")
    psd = C.nextps()
    P.op("pe", lambda e: e.matmul(psd.ap[:, 0:NKB * HL], lhsT=tri.ap[:], rhs=lfk.ap[:], start=True, stop=True), reads=[tri, lfk], writes=[psd])
    P.op("act", lambda e: e.activation(out=Dloc.ap[:], in_=psd.ap[:, 0:NKB * HL], func=AF.Copy), reads=[psd], writes=[Dloc])
    btab = [C.sb([128, NQT, NKB], F32, f"btab{h}") for h in range(HL)]
    btd = [C.sb([128, NQT, 10], F32, f"btd{h}") for h in range(HL)]
    for h in range(HL):
        ohh = oh.ap[:, h * 128:(h + 1) * 128]
        for half in range(2):
            ps = C.nextps()
            P.op("pe", lambda e, ps=ps, half=half, ohh=ohh: e.matmul(ps.ap[:], lhsT=ohh, rhs=G.ap[:, half * 8:(half + 1) * 8, :], start=True, stop=True),
                 reads=[oh, G], writes=[ps])
            for tql in range(8):
                tq = half * 8 + tql
                P.op("dve", lambda e, ps=ps, tql=tql, tq=tq, h=h: e.tensor_tensor(out=btab[h].ap[:, tq, :], in0=ps.ap[:, tql * NKB:(tql + 1) * NKB],
                                                                              in1=Dloc.ap[:, h::HL], op=ALU.add),
                     reads=[ps, Dloc], writes=[btab[h]])
        ps = C.nextps()
        P.op("pe", lambda e, ps=ps, ohh=ohh: e.matmul(ps.ap[:, 0:NQT * 10], lhsT=ohh, rhs=Gd.ap[:], start=True, stop=True), reads=[oh, Gd], writes=[ps])
        for (T_, kb_), c in ci.items():
            P.op("dve", lambda e, ps=ps, c=c, kb_=kb_, h=h: e.tensor_tensor(out=btd[h].ap[:, :, c], in0=ps.ap[:, c:NQT * 10:10],
                                                                          in1=Dloc.ap[:, kb_ * HL + h::4 * HL], op=ALU.add),
                 reads=[ps, Dloc], writes=[btd[h]])

    krot = Rot([C.sb([128, SEQ], BF16, f"k{i}") for i in range(2)])
    vrot = Rot([C.sb([128, NKB, 128], BF16, f"v{i}") for i in range(2)])
    qrot = Rot([C.sb([128, TT], BF16, f"q{i}") for i in range(4)])
    szrot = Rot([C.sb([128, TT], BF16, f"sz{i}") for i in range(4)])
    prot = Rot([C.sb([128, TT], BF16, f"p{i}") for i in range(6)])
    pdrot = Rot([C.sb([128, 128], BF16, f"pd{i}") for i in range(6)])
    worot = Rot([C.sb([128, TT], F32, f"wo{i}") for i in range(2)])
    wdrot = Rot([C.sb([128, TT], F32, f"wd{i}") for i in range(2)])
    numr = Rot([C.sb([128, TT], F32, f"num{i}") for i in range(2)])
    denr = Rot([C.sb([128, TT], F32, f"den{i}") for i in range(2)])
    n2r = Rot([C.sb([128, TT], F32, f"n2{i}") for i in range(2)])
    gor = Rot([C.sb([128, TT], BF16, f"go{i}") for i in range(2)])
    srot = Rot([C.ps[0], C.ps[1], C.ps[2], C.ps[7]])
    A_off, l_off, A_d, l_d = C.ps[3], C.ps[4], C.ps[5], C.ps[6]

    LA = 3
    pend = []

    def flush(n):
        while len(pend) > n:
            pend.pop(0)()

    for h in range(HL):
        kt = krot.next()
        vt = vrot.next()
        P.dma("sp", kt, kT, kt.ap[:], kT.ap[h])
        P.dma("sp", vt, vD, vt.ap[:], vD.ap[h].rearrange("(kb p) d -> p kb d", p=128))
        for tq in range(NQT):
            qt = qrot.next()
            szt = szrot.next()
            P.dma("sp", qt, qT, qt.ap[:], qT.ap[h, :, tq * TT:(tq + 1) * TT])
            P.dma("sp", szt, szT, szt.ap[:], szT.ap[h, :, tq * TT:(tq + 1) * TT])
            nkb = 4 * tq
            for kb in range(nkb):
                sp_ = srot.next()
                P.op("pe", lambda e, sp_=sp_, kb=kb, kt=kt, qt=qt: e.matmul(sp_.ap[:], lhsT=kt.ap[:, kb * 128:(kb + 1) * 128], rhs=qt.ap[:], start=True, stop=True),
                     reads=[kt, qt], writes=[sp_])
                pt = prot.next()
                P.op("act", lambda e, sp_=sp_, pt=pt, kb=kb, tq=tq, h=h: e.activation(out=pt.ap[:], in_=sp_.ap[:], func=AF.Exp, scale=SCALE,
                                                                               bias=btab[h].ap[:, tq, kb:kb + 1]),
                     reads=[sp_, btab[h]], writes=[pt])

                def st2(pt=pt, kb=kb, vt=vt, nkb=nkb):
                    P.op("pe", lambda e: e.matmul(A_off.ap[:], lhsT=vt.ap[:, kb, :], rhs=pt.ap[:], start=(kb == 0), stop=(kb == nkb - 1)),
                         reads=[vt, pt], writes=[A_off])
                    P.op("pe", lambda e: e.matmul(l_off.ap[:], lhsT=C.ones.ap[:], rhs=pt.ap[:], start=(kb == 0), stop=(kb == nkb - 1)),
                         reads=[C.ones, pt], writes=[l_off])
                pend.append(st2)
                flush(LA)
            for (T_, kb_), c in ci.items():
                kba = 4 * tq + kb_
                sp_ = srot.next()
                last = (kb_ == T_)
                P.op("pe", lambda e, sp_=sp_, kba=kba, T_=T_, kt=kt, qt=qt, last=last: e.matmul(sp_.ap[:, 0:128], lhsT=kt.ap[:, kba * 128:(kba + 1) * 128],
                                                                                         rhs=qt.ap[:, T_ * 128:(T_ + 1) * 128], start=True, stop=(not last)),
                     reads=[kt, qt], writes=[sp_])
                if last:
                    P.op("pe", lambda e, sp_=sp_: e.matmul(sp_.ap[:, 0:128], lhsT=ident.ap[:], rhs=cmask.ap[:], start=False, stop=True),
                         reads=[ident, cmask], writes=[sp_])
                pd = pdrot.next()
                P.op("act", lambda e, sp_=sp_, pd=pd, c=c, tq=tq, h=h: e.activation(out=pd.ap[:], in_=sp_.ap[:, 0:128], func=AF.Exp, scale=SCALE,
                                                                             bias=btd[h].ap[:, tq, c:c + 1]),
                     reads=[sp_, btd[h]], writes=[pd])

                def st2d(pd=pd, kba=kba, T_=T_, kb_=kb_, vt=vt, last=last):
                    P.op("pe", lambda e: e.matmul(A_d.ap[:, T_ * 128:(T_ + 1) * 128], lhsT=vt.ap[:, kba, :], rhs=pd.ap[:], start=(kb_ == 0), stop=last),
                         reads=[vt, pd], writes=[A_d])
                    P.op("pe", lambda e: e.matmul(l_d.ap[:, T_ * 128:(T_ + 1) * 128], lhsT=C.ones.ap[:], rhs=pd.ap[:], start=(kb_ == 0), stop=last),
                         reads=[C.ones, pd], writes=[l_d])
                pend.append(st2d)
                flush(LA)

            def comb(h=h, tq=tq, nkb=nkb, szt=szt):
                wd = wdrot.next()
                r = h * NQT + tq
                P.dma("sp", wd, scr_d, wd.ap[:], scr_d.ap[r, :].partition_broadcast(128))
                num, den = numr.next(), denr.next()
                P.op("dve", lambda e: e.tensor_tensor(out=num.ap[:], in0=A_d.ap[:], in1=wd.ap[:], op=ALU.mult), reads=[A_d, wd], writes=[num])
                P.op("dve", lambda e: e.tensor_tensor(out=den.ap[:], in0=l_d.ap[:], in1=wd.ap[:], op=ALU.mult), reads=[l_d, wd], writes=[den])
                if nkb > 0:
                    wo = worot.next()
                    P.dma("sp", wo, scr_o, wo.ap[:], scr_o.ap[r, :].partition_broadcast(128))
                    n2 = n2r.next()
                    P.op("dve", lambda e: e.tensor_tensor(out=n2.ap[:], in0=A_off.ap[:], in1=wo.ap[:], op=ALU.mult), reads=[A_off, wo], writes=[n2])
                    n3 = n2r.next()
                    P.op("dve", lambda e: e.tensor_tensor(out=n3.ap[:], in0=l_off.ap[:], in1=wo.ap[:], op=ALU.mult), reads=[l_off, wo], writes=[n3])
                    P.op("pool", lambda e: e.tensor_tensor(out=num.ap[:], in0=num.ap[:], in1=n2.ap[:], op=ALU.add), reads=[num, n2], writes=[num])
                    P.op("pool", lambda e: e.tensor_tensor(out=den.ap[:], in0=den.ap[:], in1=n3.ap[:], op=ALU.add), reads=[den, n3], writes=[den])
                P.op("dve", lambda e: e.reciprocal(out=den.ap[:], in_=den.ap[:]), reads=[den], writes=[den])
                P.op("dve", lambda e: e.tensor_tensor(out=num.ap[:], in0=num.ap[:], in1=den.ap[:], op=ALU.mult), reads=[num, den], writes=[num])
                go = gor.next()
                P.op("pool", lambda e: e.tensor_tensor(out=go.ap[:], in0=num.ap[:], in1=szt.ap[:], op=ALU.mult), reads=[num, szt], writes=[go])
                P.dma("sp", g_loc, go, g_loc.ap[(tq // 4) * 512 + h * 128:(tq // 4) * 512 + (h + 1) * 128, (tq % 4) * TT:(tq % 4 + 1) * TT], go.ap[:], semtile=go)
            pend.append(comb)
            flush(LA)
    flush(0)
    P.coll("AllGather", g_all, g_loc, g_all.ap.opt(), g_loc.ap.opt())
    C.end()


def phase_b1h(C, x1, hn_loc, hn_all, w_b, prm_d, fb_d, bsel_d, qT, kT, vD, szT, lfR):
    C.begin()
    P = C.P
    NWB = 4 * 512 + HL
    prm = C.sb([128, KD + 2], F32, "prm")
    P.dma("sp", prm, prm_d, prm.ap[:], prm_d.ap[:])
    fb = C.sb([HL, 1], F32, "fb")
    P.dma("sp", fb, fb_d, fb.ap[:], fb_d.ap[:])
    bsel = C.sb([128, 2], F32, "bsel")
    P.dma("sp", bsel, bsel_d, bsel.ap[:], bsel_d.ap[:])
    xrot = Rot([C.sb([128, KD, TT], F32, f"x{i}") for i in range(2)])
    hnr = Rot([C.sb([128, KD, TT], BF16, f"hn{i}") for i in range(2)])
    sq = C.sb([128, KD, TT], BF16, "sq")
    rstd = C.sb([128, TT], F32, "rstd")
    for t in range(NT):
        xt = xrot.next()
        hn = hnr.next()
        P.dma("sp", xt, x1, xt.ap[:], x1.ap[:, t * TT:(t + 1) * TT].rearrange("(k p) n -> p k n", p=128))
        stage_rms_hn(C, xt, TT, (prm, 0), hn, sq, rstd)
        P.dma("sp", hn_loc, hn, hn_loc.ap[:, t * TT:(t + 1) * TT].rearrange("(k p) n -> p k n", p=128), hn.ap[:], semtile=hn)
    P.coll("AllGather", hn_all, hn_loc, hn_all.ap.opt(), hn_loc.ap.opt())
    wb = C.sb([128, KD, NWB], BF16, "wb")
    for i in range(4):
        P.dma("pool", wb, w_b, wb.ap[:, :, i * 512:(i + 1) * 512], w_b.ap[:, i * 512:(i + 1) * 512].rearrange("(k p) n -> p k n", p=128))
    P.dma("pool", wb, w_b, wb.ap[:, :, 2048:NWB], w_b.ap[:, 2048:NWB].rearrange("(k p) n -> p k n", p=128))
    c0r = Rot([C.sb([128, KD, TT], BF16, f"c0{i}") for i in range(2)])
    c1r = Rot([C.sb([128, KD, TT], BF16, f"c1{i}") for i in range(2)])
    obq = Rot([C.sb([128, HL, TT], BF16, f"obq{i}") for i in range(2)])
    obk = Rot([C.sb([128, HL, TT], BF16, f"obk{i}") for i in range(2)])
    obz = Rot([C.sb([128, HL, TT], BF16, f"obz{i}") for i in range(2)])
    vstr = Rot([C.sb([128, 4, 512], BF16, f"vst{i}") for i in range(2)])
    sqh = Rot([C.sb([128, TT], BF16, f"sqh{i}") for i in range(2)])
    tmpr = Rot([C.sb([128, TT], F32, f"tmp{i}") for i in range(3)])
    t16 = C.sb([HL, TT], F32, "t16")
    lfsr = Rot([C.sb([HL, TT], F32, f"lfs{i}") for i in range(2)])
    hall = hn_all.ap.rearrange("(r k p) n -> r p k n", k=KD, p=128)
    def load_cands(tt):
        qd, col = tt // 4, (tt % 4) * TT
        c0, c1 = c0r.next(), c1r.next()
        P.dma("sp", c0, hn_all, c0.ap[:], hall[qd, :, :, col:col + TT])
        P.dma("sp", c1, hn_all, c1.ap[:], hall[4 + qd, :, :, col:col + TT])
        return c0, c1
    def select(cands):
        c0, c1 = cands
        hn = hnr.next()
        P.op("dve", lambda e, c0=c0: e.tensor_scalar(out=c0.ap[:], in0=c0.ap[:], scalar1=bsel.ap[:, 0:1], scalar2=None, op0=ALU.mult), reads=[c0, bsel], writes=[c0])
        P.op("dve", lambda e, c0=c0, c1=c1, hn=hn: e.scalar_tensor_tensor(out=hn.ap[:], in0=c1.ap[:], scalar=bsel.ap[:, 1:2], in1=c0.ap[:], op0=ALU.mult, op1=ALU.add),
             reads=[c0, c1, bsel], writes=[hn])
        return hn
    hn_next = select(load_cands(0))
    cpre = load_cands(1)
    for tt in range(NQT):
        hn = hn_next
        if tt + 1 < NQT:
            hn_next = select(cpre)
            if tt + 2 < NQT:
                cpre = load_cands(tt + 2)
        hnf = lambda k, hn=hn: hn.ap[:, k, :]
        for blk, gcol, obr, dst in ((0, KD, obq, qT), (1, KD + 1, obk, kT)):
            ob = obr.next()
            for h in range(HL):
                ps = C.nextps()
                mm_group(C, ps, (0, TT), wb, blk * 512 + h * 128, hn, hnf, KD)
                sqt = sqh.next()
                P.op("act", lambda e, ps=ps, sqt=sqt: e.activation(out=sqt.ap[:], in_=ps.ap[:], func=AF.Square), reads=[ps], writes=[sqt])
                ps2 = C.nextps()
                P.op("pe", lambda e, ps2=ps2, sqt=sqt: e.matmul(ps2.ap[:], lhsT=C.ones.ap[:], rhs=sqt.ap[:], start=True, stop=True), reads=[C.ones, sqt], writes=[ps2])
                tmp = tmpr.next()
                P.op("act", lambda e, ps2=ps2, tmp=tmp: e.activation(out=tmp.ap[:], in_=ps2.ap[:], func=AF.Ln, scale=1.0 / 128, bias=C.epsc.ap[:, 0:1]),
                     reads=[ps2, C.epsc], writes=[tmp])
                P.op("act", lambda e, tmp=tmp: e.activation(out=tmp.ap[:], in_=tmp.ap[:], func=AF.Exp, scale=-0.5), reads=[tmp], writes=[tmp])
                P.op("dve", lambda e, ps=ps, tmp=tmp, h=h, ob=ob, gcol=gcol: e.scalar_tensor_tensor(out=ob.ap[:, h, :], in0=ps.ap[:], scalar=prm.ap[:, gcol:gcol + 1],
                                                                                         in1=tmp.ap[:], op0=ALU.mult, op1=ALU.mult),
                     reads=[ps, prm, tmp], writes=[ob])
            P.dma("sp", dst, ob, dst.ap[:, :, tt * TT:(tt + 1) * TT].rearrange("h p n -> p h n"), ob.ap[:], semtile=ob)
        ob = obz.next()
        for h in range(HL):
            ps = C.nextps()
            mm_group(C, ps, (0, TT), wb, 3 * 512 + h * 128, hn, hnf, KD)
            P.op("act", lambda e, ps=ps, h=h, ob=ob: e.activation(out=ob.ap[:, h, :], in_=ps.ap[:], func=AF.Silu), reads=[ps], writes=[ob])
        P.dma("sp", szT, ob, szT.ap[:, :, tt * TT:(tt + 1) * TT].rearrange("h p n -> p h n"), ob.ap[:], semtile=ob)
        vst = vstr.next()
        for tb in range(TT // 128):
            ps = C.nextps()
            for k in range(KD):
                P.op("pe", lambda e, k=k, ps=ps, tb=tb, hn=hn: e.matmul(ps.ap[:], lhsT=hn.ap[:, k, tb * 128:(tb + 1) * 128], rhs=wb.ap[:, k, 1024:1536],
                                                                 start=(k == 0), stop=(k == KD - 1)),
                     reads=[hn, wb], writes=[ps])
            if tb % 2 == 0:
                P.op("act", lambda e, ps=ps, tb=tb, vst=vst: e.activation(out=vst.ap[:, tb, :], in_=ps.ap[:], func=AF.Copy), reads=[ps], writes=[vst])
            else:
                P.op("dve", lambda e, ps=ps, tb=tb, vst=vst: e.tensor_copy(out=vst.ap[:, tb, :], in_=ps.ap[:]), reads=[ps], writes=[vst])
        for h in range(HL):
            P.dma("sp", vD, vst, vD.ap[h, tt * TT:(tt + 1) * TT, :].rearrange("(tb p) d -> p tb d", p=128), vst.ap[:, :, h * 128:(h + 1) * 128], semtile=vst)
        ps = C.nextps()
        for k in range(KD):
            P.op("pe", lambda e, k=k, ps=ps, hn=hn: e.matmul(ps.ap[0:HL, :], lhsT=wb.ap[:, k, 2048:2048 + HL], rhs=hn.ap[:, k, :], start=(k == 0), stop=(k == KD - 1)),
                 reads=[wb, hn], writes=[ps])
        lfs = lfsr.next()
        P.op("act", lambda e, ps=ps: e.activation(out=t16.ap[:], in_=ps.ap[0:HL, :], func=AF.Sigmoid, bias=fb.ap[:, 0:1]), reads=[ps, fb], writes=[t16])
        P.op("act", lambda e, lfs=lfs: e.activation(out=lfs.ap[:], in_=t16.ap[:], func=AF.Ln), reads=[t16], writes=[lfs])
        P.dma("sp", lfR, lfs, lfR.ap[:, tt * TT:(tt + 1) * TT], lfs.ap[:], semtile=lfs)
    C.end()


def phase_b3(C, x1, g_all, gsel_d, w_out, x2):
    C.begin()
    P = C.P
    gsel = C.sb([128, 8], F32, "gsel")
    P.dma("sp", gsel, gsel_d, gsel.ap[:], gsel_d.ap[:])
    xrot = Rot([C.sb([128, KD, TT], F32, f"x{i}") for i in range(2)])
    candr = Rot([C.sb([128, KE, TT], BF16, f"cand{i}") for i in range(3)])
    gacc = Rot([C.sb([128, KE, TT], BF16, f"gacc{i}") for i in range(2)])
    worot = Rot([C.sb([128, KE, 256], BF16, f"wo{i}") for i in range(3)])
    gv = g_all.ap.rearrange("(bb hq rr kk p) n -> bb rr p hq kk n", bb=2, hq=4, rr=4, kk=4, p=128)
    for t in range(NT):
        xt = xrot.next()
        P.dma("sp", xt, x1, xt.ap[:], x1.ap[:, t * TT:(t + 1) * TT].rearrange("(k p) n -> p k n", p=128))
        g = gacc.next()
        for ci in range(8):
            bb, rr = ci // 4, ci % 4
            cand = candr.next()
            for hq in range(4):
                P.dma("sp", cand, g_all, cand.ap[:, hq * 4:(hq + 1) * 4, :], gv[bb, rr, :, hq, :, t * TT:(t + 1) * TT])
            if ci == 0:
                P.op("dve", lambda e, cand=cand, g=g, ci=ci: e.tensor_scalar(out=g.ap[:], in0=cand.ap[:], scalar1=gsel.ap[:, ci:ci + 1], scalar2=None, op0=ALU.mult),
                     reads=[cand, gsel], writes=[g])
            else:
                P.op("dve", lambda e, cand=cand, g=g, ci=ci: e.scalar_tensor_tensor(out=g.ap[:], in0=cand.ap[:], scalar=gsel.ap[:, ci:ci + 1], in1=g.ap[:], op0=ALU.mult, op1=ALU.add),
                     reads=[cand, gsel, g], writes=[g])
        stage_out_proj(C, w_out, worot, g, xt, xt)
        P.dma("sp", x2, xt, x2.ap[:, t * TT:(t + 1) * TT].rearrange("(k p) n -> p k n", p=128), xt.ap[:], semtile=xt)
    C.end()


def make_halo_exchange(C, x_prev, hb_loc, hb_all, hsel_d, HPX):
    def halo_fn(xh):
        P = C.P
        tail = C.sb([128, KD, 32], F32, "tail")
        P.dma("sp", tail, x_prev, tail.ap[:], x_prev.ap[:, TPC - 32:TPC].rearrange("(k p) n -> p k n", p=128))
        P.dma("sp", hb_loc, tail, hb_loc.ap.rearrange("(k p) n -> p k n", p=128), tail.ap[:], semtile=tail)
        P.coll("AllGather", hb_all, hb_loc, hb_all.ap.opt(), hb_loc.ap.opt())
        cand = C.sb([128, NCORES, KD, 32], F32, "hcand")
        P.dma("sp", cand, hb_all, cand.ap[:], hb_all.ap.rearrange("(r k p) n -> p r k n", r=NCORES, k=KD, p=128))
        hsel = C.sb([128, NCORES], F32, "hsel")
        P.dma("sp", hsel, hsel_d, hsel.ap[:], hsel_d.ap[:])
        acc = C.sb([128, KD, 32], F32, "hacc")
        P.op("dve", lambda e: e.tensor_scalar(out=acc.ap[:], in0=cand.ap[:, 0], scalar1=hsel.ap[:, 0:1], scalar2=None, op0=ALU.mult), reads=[cand, hsel], writes=[acc])
        for r in range(1, NCORES):
            P.op("dve", lambda e, r=r: e.scalar_tensor_tensor(out=acc.ap[:], in0=cand.ap[:, r], scalar=hsel.ap[:, r:r + 1], in1=acc.ap[:], op0=ALU.mult, op1=ALU.add),
                 reads=[cand, hsel, acc], writes=[acc])
        P.op("dve", lambda e: e.tensor_copy(out=xh.ap[:], in_=acc.ap[:, :, 32 - HPX:32]), reads=[acc], writes=[xh])
    return halo_fn


def build_fused():
    nc = bass.Bass("TRN2", target_bir_lowering=False)
    with ExitStack() as es:
        C = Ctx(nc, es)
        NPA = KD + KE * 31 + 3 * KE
        NPC = KD + KE * 3 + 3 * KE
        xT = C.dram("xT", [D, TPC], F32, "ExternalInput")
        xh0 = C.dram("xh0", [D, 32], F32, "ExternalInput")
        a0_win = C.dram("a0_win", [D, 3 * E], F32, "ExternalInput")
        a0_wout = C.dram("a0_wout", [E, D], F32, "ExternalInput")
        a0_prm = C.dram("a0_prm", [128, NPA], F32, "ExternalInput")
        a1_win = C.dram("a1_win", [D, 3 * E], F32, "ExternalInput")
        a1_wout = C.dram("a1_wout", [E, D], F32, "ExternalInput")
        a1_prm = C.dram("a1_prm", [128, NPA], F32, "ExternalInput")
        c_win = C.dram("c_win", [D, 4 * E], F32, "ExternalInput")
        c_wout = C.dram("c_wout", [E, D], F32, "ExternalInput")
        c_prm = C.dram("c_prm", [128, NPC], F32, "ExternalInput")
        b_w = C.dram("b_w", [D, 4 * 512 + HL], F32, "ExternalInput")
        b_wout = C.dram("b_wout", [E, D], F32, "ExternalInput")
        b_prm = C.dram("b_prm", [128, KD + 2], F32, "ExternalInput")
        b_fb = C.dram("b_fb", [HL, 1], F32, "ExternalInput")
        bsel = C.dram("bsel", [128, 2], F32, "ExternalInput")
        gsel = C.dram("gsel", [128, 8], F32, "ExternalInput")
        hsel = C.dram("hsel", [128, NCORES], F32, "ExternalInput")
        C.id_d = C.dram("ident", [128, 128], F32, "ExternalInput")
        cm_d = C.dram("cmask", [128, 128], F32, "ExternalInput")
        tri_d = C.dram("tri", [128, 128], F32, "ExternalInput")
        oh_d = C.dram("oh", [HL, HL * 128], F32, "ExternalInput")
        yT = C.dram("yT", [D, TPC], F32, "ExternalOutput")
        x1 = C.dram("x1", [D, TPC], F32)
        x2 = C.dram("x2", [D, TPC], F32)
        x3 = C.dram("x3", [D, TPC], F32)
        hn_loc = C.dram("hn_loc", [D, TPC], BF16)
        hn_all = C.dram("hn_all", [NCORES * D, TPC], BF16)
        qh = C.dram("qh", [HL, 128, SEQ], BF16)
        kh = C.dram("kh", [HL, 128, SEQ], BF16)
        szh = C.dram("szh", [HL, 128, SEQ], BF16)
        vh = C.dram("vh", [HL, SEQ, 128], BF16)
        lfh = C.dram("lfh", [HL, SEQ], F32)
        scr_o = C.dram("scr_o", [HL * NQT, TT], F32)
        scr_d = C.dram("scr_d", [HL * NQT, TT], F32)
        g_loc = C.dram("g_loc", [4 * 512, TPC], BF16)
        g_all = C.dram("g_all", [NCORES * 4 * 512, TPC], BF16)
        hbc_loc = C.dram("hbc_loc", [D, 32], F32)
        hbc_all = C.dram("hbc_all", [NCORES * D, 32], F32)
        hba_loc = C.dram("hba_loc", [D, 32], F32)
        hba_all = C.dram("hba_all", [NCORES * D, 32], F32)

        C.begin()
        stage_consts(C)
        C.end()

        def halo0(xh):
            C.P.dma("sp", xh, xh0, xh.ap[:], xh0.ap.rearrange("(k p) n -> p k n", p=128))
        phase_conv(C, "A", xT, x1, halo0, a0_win, a0_wout, a0_prm)
        phase_b1h(C, x1, hn_loc, hn_all, b_w, b_prm, b_fb, bsel, qh, kh, vh, szh, lfh)
        phase_b2(C, qh, kh, vh, szh, lfh, g_loc, g_all, cm_d, tri_d, oh_d, scr_o, scr_d)
        phase_b3(C, x1, g_all, gsel, b_wout, x2)
        phase_conv(C, "C", x2, x3, make_halo_exchange(C, x2, hbc_loc, hbc_all, hsel, 2), c_win, c_wout, c_prm)
        phase_conv(C, "A", x3, yT, make_halo_exchange(C, x3, hba_loc, hba_all, hsel, 32), a1_win, a1_wout, a1_prm)
    return nc


_CACHE = {}


def _pack_cols(v, nchunks):
    return np.ascontiguousarray(v.reshape(nchunks, 128).T)


def _conv_prm(norm_g, conv_w, conv_b=None, ln_g=None, ln_b=None):
    KW = conv_w.shape[0]
    cols = [_pack_cols(norm_g, KD)]
    cols.append(np.ascontiguousarray(conv_w.T).reshape(KE, 128, KW).transpose(1, 0, 2).reshape(128, KE * KW))
    z16 = np.zeros((128, KE), np.float32)
    cols.append(_pack_cols(conv_b, KE) if conv_b is not None else z16)
    cols.append(_pack_cols(ln_g, KE) if ln_g is not None else z16)
    cols.append(_pack_cols(ln_b, KE) if ln_b is not None else z16)
    return np.ascontiguousarray(np.concatenate(cols, axis=1).astype(np.float32))


def kernel(x, a_norm, a_w_in, a_conv_w, a_conv_b, a_ln_g, a_ln_b, a_w_out,
           b_norm, b_w_in, b_f_bias, b_q_norm, b_k_norm, b_w_out,
           c_norm, c_w_in, c_conv_w, c_w_out):
    f = lambda a: np.ascontiguousarray(np.asarray(a, dtype=np.float32))
    x = f(x)
    xT_full = np.ascontiguousarray(x.reshape(NTOK, D).T)
    a_w_in, a_w_out, b_w_in, b_w_out, c_w_in, c_w_out = f(a_w_in), f(a_w_out), f(b_w_in), f(b_w_out), f(c_w_in), f(c_w_out)
    a0_prm = _conv_prm(f(a_norm)[0], f(a_conv_w)[0], f(a_conv_b)[0], f(a_ln_g)[0], f(a_ln_b)[0])
    a1_prm = _conv_prm(f(a_norm)[1], f(a_conv_w)[1], f(a_conv_b)[1], f(a_ln_g)[1], f(a_ln_b)[1])
    c_prm = _conv_prm(f(c_norm)[0], f(c_conv_w)[0])
    b_prm = np.ascontiguousarray(np.concatenate([_pack_cols(f(b_norm)[0], KD), f(b_q_norm)[0].reshape(128, 1), f(b_k_norm)[0].reshape(128, 1)], axis=1))
    p_ = np.arange(128)
    cmask = np.where(p_[None, :] < p_[:, None], np.float32(-30000.0), np.float32(0.0)).astype(np.float32)
    ident = np.eye(128, dtype=np.float32)
    tri = (p_[:, None] > p_[None, :]).astype(np.float32)
    oh = np.zeros((HL, HL * 128), np.float32)
    for h in range(HL):
        oh[h, h * 128:(h + 1) * 128] = 1.0
    wb_in = b_w_in[0]
    in_maps = []
    for c in range(NCORES):
        b, r = c // 4, c % 4
        t0 = c * TPC
        xh0 = np.zeros((D, 32), np.float32)
        if r != 0:
            xh0 = np.ascontiguousarray(xT_full[:, t0 - 32:t0])
        hcols = slice(r * 512, (r + 1) * 512)
        b_w = np.ascontiguousarray(np.concatenate([wb_in[:, 0:E][:, hcols], wb_in[:, E:2 * E][:, hcols], wb_in[:, 2 * E:3 * E][:, hcols],
                                                   wb_in[:, 3 * E:4 * E][:, hcols], wb_in[:, 4 * E + r * HL:4 * E + (r + 1) * HL]], axis=1))
        bsel = np.zeros((128, 2), np.float32); bsel[:, b] = 1.0
        gsel = np.zeros((128, 8), np.float32); gsel[:, b * 4 + r] = 1.0
        hsel = np.zeros((128, NCORES), np.float32)
        if r != 0:
            hsel[:, c - 1] = 1.0
        in_maps.append({
            "xT": np.ascontiguousarray(xT_full[:, t0:t0 + TPC]), "xh0": xh0,
            "a0_win": a_w_in[0], "a0_wout": a_w_out[0], "a0_prm": a0_prm,
            "a1_win": a_w_in[1], "a1_wout": a_w_out[1], "a1_prm": a1_prm,
            "c_win": c_w_in[0], "c_wout": c_w_out[0], "c_prm": c_prm,
            "b_w": b_w, "b_wout": b_w_out[0], "b_prm": b_prm,
            "b_fb": np.ascontiguousarray(f(b_f_bias)[0][r * HL:(r + 1) * HL].reshape(HL, 1)),
            "bsel": bsel, "gsel": gsel, "hsel": hsel,
            "ident": ident, "cmask": cmask, "tri": tri, "oh": oh,
        })
    if "nc" not in _CACHE:
        _CACHE["nc"] = build_fused()
    res = run_bass_kernel_spmd(_CACHE["nc"], in_maps, core_ids=list(range(NCORES)))
    yT = np.concatenate([r_["yT"] for r_ in res.results], axis=1)
    return np.ascontiguousarray(yT.T).reshape(BATCH, SEQ, D)
```
